# Optimizing a Trainium2 kernel written in Bass

```python
import jax, jax.numpy as jnp
from jax import lax
import numpy as np

D_MODEL = 1024
BATCH = 8
SEQ = 4096
DEPTH = 1
DEC_BATCH = 32
DEC_SEQ = 8
PAST_LEN = 16384
PAGE_SIZE = 128

CONV_CH = D_MODEL
CONV_WIDTH = 31
N_SLOTS = 8
HEAD_DIM = 64
GROUPS = ((128, 1), (512, 4), (2048, 16))
N_GROUPS = len(GROUPS)
ATT_W = N_SLOTS * HEAD_DIM
QKV_W = N_GROUPS * ATT_W
Q_BLK = 128
ALPHA = (2.0 * DEPTH) ** 0.25
BETA = (8.0 * DEPTH) ** -0.25
LN_EPS = 1e-5
SPLITS = (CONV_CH, CONV_CH, CONV_CH, QKV_W, QKV_W, QKV_W, ATT_W, D_MODEL, D_MODEL)
IN_W = sum(SPLITS)

kernel_name = "gated_conformer_dilated_alibi_deepnorm_step"


def layer_norm(x, g, b):
    xf = x.astype(jnp.float32)
    mu = xf.mean(-1, keepdims=True)
    var = jnp.square(xf - mu).mean(-1, keepdims=True)
    y = (xf - mu) * lax.rsqrt(var + LN_EPS) * g.astype(jnp.float32) + b.astype(jnp.float32)
    return y.astype(x.dtype)


def alibi_slopes():
    return 2.0 ** (-8.0 * jnp.arange(1, N_SLOTS + 1, dtype=jnp.float32) / N_SLOTS)


def in_projection(x, w_in, b_in):
    h = jnp.einsum('bsd,de->bse', x, w_in) + b_in
    offs = np.cumsum(SPLITS)[:-1].tolist()
    a_val, a_gate, z_a, q, k, v, z_b, g_a, g_b = jnp.split(h, offs, axis=-1)
    u = a_val * jax.nn.sigmoid(a_gate)
    B, S = x.shape[0], x.shape[1]
    shp = (B, S, N_GROUPS, N_SLOTS, HEAD_DIM)
    q = q.reshape(shp) * (HEAD_DIM ** -0.5)
    return u, z_a, q, k.reshape(shp), v.reshape(shp), z_b, g_a, g_b


def depthwise_valid(u_hist, conv_w):
    return lax.conv_general_dilated(u_hist, conv_w[:, None, :], window_strides=(1,), padding='VALID',
                                    dimension_numbers=('NWC', 'WIO', 'NWC'), feature_group_count=CONV_CH)


def dilated_prompt(q, k, v, window, dil, slopes):
    B, S, H, Dh = q.shape
    steps = window // dil
    span = dil * Q_BLK
    s_pad = -(-S // span) * span
    nb = s_pad // span
    pad = ((0, 0), (0, s_pad - S), (0, 0), (0, 0))
    blk = lambda t: jnp.pad(t, pad).reshape(B, nb, Q_BLK, dil, H, Dh)
    qb, kb, vb = blk(q), blk(k), blk(v)

    def with_prev(t):
        prev = jnp.pad(t[:, :-1], ((0, 0), (1, 0), (0, 0), (0, 0), (0, 0), (0, 0)))
        return jnp.concatenate([prev, t], axis=2)

    kk, vv = with_prev(kb), with_prev(vb)
    s = jnp.einsum('bnirhd,bnjrhd->bnrhij', qb, kk, preferred_element_type=jnp.float32)
    i = jnp.arange(Q_BLK)[:, None]
    j = jnp.arange(2 * Q_BLK)[None, :]
    delta = i + Q_BLK - j
    n = jnp.arange(nb)[:, None, None]
    valid = (delta >= 0) & (delta <= steps) & ((n > 0) | (j >= Q_BLK))
    bias = -slopes[:, None, None] * (delta * dil).astype(jnp.float32)
    s = jnp.where(valid[None, :, None, None], s + bias, -jnp.inf)
    lse = jax.nn.logsumexp(s, axis=-1)
    p = jnp.exp(s - lse[..., None]).astype(v.dtype)
    o = jnp.einsum('bnrhij,bnjrhd->bnirhd', p, vv)
    o = o.reshape(B, s_pad, H, Dh)[:, :S]
    lse = lse.transpose(0, 1, 4, 2, 3).reshape(B, s_pad, H)[:, :S]
    return o, lse


def dilated_sample(q, kv_buf, k, v, window, dil, slopes):
    T = q.shape[1]
    wb = kv_buf.shape[1]
    steps = window // dil
    kcat = jnp.concatenate([kv_buf[:, :, 0], k], axis=1)
    vcat = jnp.concatenate([kv_buf[:, :, 1], v], axis=1)
    jstep = jnp.arange(steps + 1)
    idx = wb + jnp.arange(T)[:, None] - jstep[None, :] * dil
    valid = idx >= 0
    idxc = jnp.maximum(idx, 0)
    kg = kcat[:, idxc]
    vg = vcat[:, idxc]
    s = jnp.einsum('bthd,btkhd->bhtk', q, kg, preferred_element_type=jnp.float32)
    s = s - slopes[:, None, None] * (jstep * dil).astype(jnp.float32)[None, None, :]
    s = jnp.where(valid, s, -jnp.inf)
    lse = jax.nn.logsumexp(s, axis=-1)
    p = jnp.exp(s - lse[..., None]).astype(v.dtype)
    o = jnp.einsum('bhtk,btkhd->bthd', p, vg)
    new_buf = jnp.stack([kcat, vcat], axis=2)[:, -wb:]
    return o, lse.transpose(0, 2, 1), new_buf


def combine_groups(outs, lses):
    o = jnp.stack(outs, axis=2)
    w = jax.nn.softmax(jnp.stack(lses, axis=2), axis=2)
    o = jnp.sum(w[..., None].astype(o.dtype) * o, axis=2)
    return o.reshape(o.shape[0], o.shape[1], ATT_W)


def merge_and_norm(x, conv_raw, attn, z_a, z_b, g_a, g_b, conv_b, conv_ln_g, conv_ln_b,
                   w_a, w_b, w_out, ln_g, ln_b):
    c = jax.nn.silu(layer_norm(conv_raw + conv_b, conv_ln_g, conv_ln_b))
    pa = jnp.einsum('bsc,cd->bsd', c * jax.nn.silu(z_a), w_a)
    pb = jnp.einsum('bsc,cd->bsd', attn * jax.nn.silu(z_b), w_b)
    m = jax.nn.sigmoid(g_a) * pa + jax.nn.sigmoid(g_b) * pb
    o = jnp.einsum('bsd,de->bse', m, w_out)
    return layer_norm(ALPHA * x + o, ln_g, ln_b)


def setup_inputs(seed: int = 0) -> dict:
    key = jax.random.key(seed)
    ks = jax.random.split(key, 20)
    f32 = jnp.float32
    nrm = lambda k, shp: jax.random.normal(k, shp, f32)
    x_prompt = nrm(ks[0], (BATCH, SEQ, D_MODEL))
    x_sample = nrm(ks[1], (DEC_BATCH, DEC_SEQ, D_MODEL))
    caches = []
    for g, (win, dil) in enumerate(GROUPS):
        wb = min(win, PAST_LEN)
        caches.append(nrm(ks[2 + g], (DEC_BATCH, wb, 2, N_SLOTS, HEAD_DIM)))
    state_conv = 0.5 * nrm(ks[5], (DEC_BATCH, CONV_WIDTH - 1, CONV_CH))
    col_scale = np.ones((IN_W,), np.float32)
    v_start = 3 * CONV_CH + 2 * QKV_W
    col_scale[v_start:v_start + QKV_W] = BETA
    w_in = nrm(ks[6], (D_MODEL, IN_W)) * (D_MODEL ** -0.5) * jnp.asarray(col_scale)
    b_in = 0.01 * nrm(ks[7], (IN_W,))
    conv_w = nrm(ks[8], (CONV_WIDTH, CONV_CH)) * (CONV_WIDTH ** -0.5)
    conv_b = 0.01 * nrm(ks[9], (CONV_CH,))
    conv_ln_g = 1.0 + 0.01 * nrm(ks[10], (CONV_CH,))
    conv_ln_b = 0.01 * nrm(ks[11], (CONV_CH,))
    w_a = nrm(ks[12], (CONV_CH, D_MODEL)) * (CONV_CH ** -0.5) * BETA
    w_b = nrm(ks[13], (ATT_W, D_MODEL)) * (ATT_W ** -0.5) * BETA
    w_out = nrm(ks[14], (D_MODEL, D_MODEL)) * (D_MODEL ** -0.5) * BETA
    ln_g = 1.0 + 0.01 * nrm(ks[15], (D_MODEL,))
    ln_b = 0.01 * nrm(ks[16], (D_MODEL,))
    return {'x_prompt': x_prompt, 'x_sample': x_sample,
            'cache_kv_w128': caches[0], 'cache_kv_w512': caches[1], 'cache_kv_w2048': caches[2],
            'state_conv': state_conv,
            'w_in': w_in, 'b_in': b_in, 'conv_w': conv_w, 'conv_b': conv_b,
            'conv_ln_g': conv_ln_g, 'conv_ln_b': conv_ln_b,
            'w_a': w_a, 'w_b': w_b, 'w_out': w_out, 'ln_g': ln_g, 'ln_b': ln_b}


def reference(x_prompt, x_sample, cache_kv_w128, cache_kv_w512, cache_kv_w2048, state_conv,
              w_in, b_in, conv_w, conv_b, conv_ln_g, conv_ln_b, w_a, w_b, w_out, ln_g, ln_b):
    slopes = alibi_slopes()
    assert DEPTH == 1
    y_p, y_s = x_prompt, x_sample
    for _layer in range(DEPTH):
        S = y_p.shape[1]
        u, z_a, q, k, v, z_b, g_a, g_b = in_projection(y_p, w_in, b_in)
        u_hist = jnp.pad(u, ((0, 0), (CONV_WIDTH - 1, 0), (0, 0)))
        conv_raw = depthwise_valid(u_hist, conv_w)
        conv_p = u_hist[:, -(CONV_WIDTH - 1):]
        outs, lses, kv_p = [], [], []
        for g, (win, dil) in enumerate(GROUPS):
            o, l = dilated_prompt(q[:, :, g], k[:, :, g], v[:, :, g], win, dil, slopes)
            outs.append(o)
            lses.append(l)
            keep = min(win, S)
            kv_p.append(jnp.stack([k[:, S - keep:, g], v[:, S - keep:, g]], axis=2))
        attn = combine_groups(outs, lses)
        y_p = merge_and_norm(y_p, conv_raw, attn, z_a, z_b, g_a, g_b, conv_b, conv_ln_g, conv_ln_b,
                             w_a, w_b, w_out, ln_g, ln_b)

        u, z_a, q, k, v, z_b, g_a, g_b = in_projection(y_s, w_in, b_in)
        u_hist = jnp.concatenate([state_conv.astype(u.dtype), u], axis=1)
        conv_raw = depthwise_valid(u_hist, conv_w)
        conv_s = u_hist[:, -(CONV_WIDTH - 1):]
        bufs = (cache_kv_w128, cache_kv_w512, cache_kv_w2048)
        outs, lses, kv_s = [], [], []
        for g, (win, dil) in enumerate(GROUPS):
            o, l, nb_ = dilated_sample(q[:, :, g], bufs[g].astype(k.dtype), k[:, :, g], v[:, :, g], win, dil, slopes)
            outs.append(o)
            lses.append(l)
            kv_s.append(nb_)
        attn = combine_groups(outs, lses)
        y_s = merge_and_norm(y_s, conv_raw, attn, z_a, z_b, g_a, g_b, conv_b, conv_ln_g, conv_ln_b,
                             w_a, w_b, w_out, ln_g, ln_b)
    return (y_p, y_s, kv_p[0], kv_p[1], kv_p[2], conv_p, kv_s[0], kv_s[1], kv_s[2], conv_s)
```

```python
import numpy as np
import concourse.bass as bass
import concourse.mybir as mybir
from concourse.bass_utils import run_bass_kernel_spmd
from contextlib import ExitStack

F32 = mybir.dt.float32
BF16 = mybir.dt.bfloat16
AF = mybir.ActivationFunctionType
ALU = mybir.AluOpType

S = 4096
GROUPS = ((128, 1), (512, 4), (2048, 16))
ALPHA = 2.0 ** 0.25
LN_EPS = 1e-5
NEG = -30000.0
NCH = (1, 4, 16)
CH0 = (0, 1, 5)
KD = 31
SEG = 3000


class Res:
    __slots__ = ("name", "w", "r", "dsem", "dcnt", "ssem", "scnt", "qsem", "qcnt")

    def __init__(self, name):
        self.name = name
        self.w = None
        self.r = {}
        self.dsem = None
        self.dcnt = 0
        self.ssem = None
        self.scnt = 0
        self.qsem = None
        self.qcnt = 0


class KB:
    CE = ("pe", "act", "dve", "pool")

    def __init__(self, nc, es):
        self.nc = nc
        self.es = es
        self.E = {"pe": nc.tensor, "act": nc.scalar, "dve": nc.vector, "pool": nc.gpsimd, "sp": nc.sync}
        self.nsig = {e: 0 for e in self.CE}
        self.sems = {e: [] for e in self.CE}
        self.seen = {e: {} for e in self.E}
        self.nsem = 0
        self.dma_ev = {}

    def newsem(self):
        self.nsem += 1
        return self.es.enter_context(self.nc.semaphore(f"s{self.nsem}"))

    def _sem_for(self, e, n):
        idx = (n - 1) // SEG
        while len(self.sems[e]) <= idx:
            self.sems[e].append(self.newsem())
        return self.sems[e][idx], (n - 1) % SEG + 1

    def _need(self, e, ev):
        kind, key, val = ev
        k = key if kind == "c" else ("d", id(key))
        if self.seen[e].get(k, 0) >= val:
            return
        self.seen[e][k] = val
        if kind == "c":
            sem, v = self._sem_for(key, val)
            self.E[e].wait_ge(sem, v)
        else:
            self.E[e].wait_ge(key, val)

    def _dep(self, e, ev, raw):
        if ev[0] == "c" and ev[1] == e and not raw and e == "pe":
            return
        self._need(e, ev)

    def op(self, e, fn, reads=(), writes=(), sig=True):
        for r in reads:
            if r.w is not None:
                self._dep(e, r.w, True)
        for w in writes:
            if w.w is not None:
                self._dep(e, w.w, False)
            for x in w.r.values():
                self._dep(e, x, False)
        ins = fn(self.E[e])
        if sig:
            self.nsig[e] += 1
            sem, _ = self._sem_for(e, self.nsig[e])
            ins.then_inc(sem, 1)
            ev = ("c", e, self.nsig[e])
        else:
            ev = ("c", e, self.nsig[e] + 1)
        for r in reads:
            r.r[e] = ev
        for w in writes:
            w.w = ev
            w.r = {}

    def dma(self, q, out, in_, reads=(), writes=(), store=False):
        own = reads[0] if store else writes[0]
        if store:
            if own.ssem is None:
                own.ssem = self.newsem()
            sem = own.ssem
            own.scnt += 16
            val = own.scnt
        elif q == "pool":
            if own.qsem is None:
                own.qsem = self.newsem()
            sem = own.qsem
            own.qcnt += 16
            val = own.qcnt
        else:
            if own.dsem is None:
                own.dsem = self.newsem()
            sem = own.dsem
            own.dcnt += 16
            val = own.dcnt
        for r in reads:
            if r.w is not None:
                self._dep(q, r.w, True)
        for w in writes:
            if w.w is not None and not (w.w[0] == "d" and w.w[1] is sem):
                self._dep(q, w.w, False)
            for x in w.r.values():
                self._dep(q, x, False)
        self.E[q].dma_start(out=out, in_=in_).then_inc(sem, 16)
        ev = ("d", sem, val)
        for r in reads:
            r.r[("d", id(sem))] = ev
        for w in writes:
            w.w = ev
            w.r = {}
        self.dma_ev[id(sem)] = ev
        return ev

    def barrier(self):
        for e in self.E:
            for f in self.CE:
                if f != e and self.nsig[f] > 0:
                    self._need(e, ("c", f, self.nsig[f]))
            for ev in self.dma_ev.values():
                self._need(e, ev)

    def finish(self):
        for ev in self.dma_ev.values():
            self._need("sp", ev)


def build_program():
    nc = bass.Bass("TRN2", target_bir_lowering=False)

    def din(name, shape, dt=F32):
        return nc.dram_tensor(name, list(shape), dt, kind="ExternalInput").ap()

    def dout(name, shape):
        return nc.dram_tensor(name, list(shape), F32, kind="ExternalOutput").ap()

    x = din("x", [4096, 1024])
    xT = din("xT", [128, 8, 4096])
    xs = din("xs", [32, 1024])
    xsT = din("xsT", [128, 8, 32])
    cache = [din("c0", [4, 128, 2, 512]), din("c1", [4, 512, 2, 512]), din("c2", [4, 2048, 2, 512])]
    ckT = [din("k0", [4, 512, 128]), din("k1", [4, 512, 512]), din("k2", [4, 512, 2048])]
    scT = din("scT", [128, 8, 4, 30])
    WIN = din("WIN", [80, 128, 8, 128])
    WA = din("WA", [8, 128, 8, 128])
    WB = din("WB", [8, 128, 4, 128])
    WO = din("WO", [128, 8, 1024])
    bcol_d = din("bcol", [128, 80])
    brep_d = din("brep", [128, 24, 4, 128])
    convw_d = din("convw", [128, 8, 31])
    cvec_d = din("cvec", [128, 3, 8])
    lngb_d = din("lngb", [128, 2, 1024])
    biasP_d = din("biasP", [4, 128, 3, 2, 2, 128])
    biasS_d = din("biasS", [4, 128, 21, 2, 4, 8])
    biasN_d = din("biasN", [4, 8, 3, 2, 4, 8])
    ident_d = din("ident", [128, 128])

    y = dout("y", [4096, 1024])
    ys = dout("ys", [32, 1024])
    kvp = [dout("kvp0", [128, 2, 512]), dout("kvp1", [512, 2, 512]), dout("kvp2", [2048, 2, 512])]
    convp = dout("convp", [128, 8, 30])
    kvs = [dout("kvs0", [4, 128, 2, 512]), dout("kvs1", [4, 512, 2, 512]), dout("kvs2", [4, 2048, 2, 512])]
    convs = dout("convs", [128, 8, 4, 30])

    WSC = nc.dram_tensor("wsc", [56, 128, 8, 128], BF16, kind="Internal").ap()
    ABS = nc.dram_tensor("absc", [128, 4, 4096], BF16, kind="Internal").ap()
    ABSs = nc.dram_tensor("abssc", [128, 4, 32], BF16, kind="Internal").ap()

    with ExitStack() as es:
        kb = KB(nc, es)

        def sb(name, shape, dt, st=es):
            return st.enter_context(nc.sbuf_tensor("s_" + name, list(shape), dt))

        PS = [es.enter_context(nc.psum_tensor(f"ps{i}", [128, 512], F32)) for i in range(8)]
        R_PS = [Res(f"ps{i}") for i in range(8)]
        xTb = sb("xTb", [128, 8, 4096], BF16)
        R_xT = [Res(f"xT{i}") for i in range(8)]
        xsTb = sb("xsTb", [128, 8, 32], BF16)
        R_xsT = Res("xsT")
        NSTG, NSLB = 2, 8
        slb = [sb(f"slb{i}", [128, 8, 128], BF16) for i in range(NSLB)]
        R_slb = [Res(f"slb{i}") for i in range(NSLB)]
        bcol = sb("bcol", [128, 80], F32)
        R_bcol = Res("bcol")
        bq8 = sb("bq8", [128, 12], F32)
        R_bq8 = Res("bq8")
        ones = sb("ones", [128, 128], BF16)
        R_ones = Res("ones")
        esA = ExitStack()
        R_wsc = [Res(f"wsc{i}") for i in range(56)]
        R_abs = [[Res(f"abs{h}_{t}") for t in range(8)] for h in range(4)]
        R_abss = [Res(f"abss{h}") for h in range(4)]
        cnt = {"stg": 0, "slb": 0, "pj": 0}

        def rngs(lo, hi):
            return list(range(lo // 512, (hi - 1) // 512 + 1))

        deferred = []

        def slab_f32(src, defer=False):
            j = cnt["slb"] % NSLB
            cnt["slb"] += 1
            kb.dma("pool", slb[j][:], src, writes=[R_slb[j]])
            if defer:
                return (slb[j], R_slb[j]), (lambda: None), (lambda: None)
            return slb[j], R_slb[j]

        def run_deferred(n=1):
            for _ in range(n):
                if deferred:
                    deferred.pop(0)()

        def slab_scr(idx, nk=8):
            j = cnt["slb"] % NSLB
            cnt["slb"] += 1
            kb.dma("sp", slb[j][:, 0:nk, :], WSC[idx, :, 0:nk, :], reads=[R_wsc[idx]], writes=[R_slb[j]])
            return slb[j], R_slb[j]

        scrb = {}

        def to_scratch(idx, src, nk=8):
            kb.dma("pool", scrb["b"][:, 0:nk, :], src, writes=[scrb["Rb"]])
            kb.dma("sp", WSC[idx, :, 0:nk, :], scrb["b"][:, 0:nk, :], reads=[scrb["Rb"]], writes=[R_wsc[idx]], store=True)

        scr_jobs = []
        for c in range(8):
            scr_jobs.append((c, WIN[c], 8))
            scr_jobs.append((8 + c, WIN[8 + c], 8))
            scr_jobs.append((16 + c, WIN[16 + c], 8))
        for c in range(8):
            scr_jobs.append((24 + c, WIN[64 + c], 8))
            scr_jobs.append((32 + c, WIN[72 + c], 8))
            scr_jobs.append((40 + c, WA[c], 8))
            scr_jobs.append((48 + c, WB[c], 4))

        def pj_bank():
            b = 6 + cnt["pj"] % 2
            cnt["pj"] += 1
            return b

        def proj_fm(bank, n, slab, Rslab, rhs_fn, rhs_res, nk=8):
            for kc in range(nk):
                kb.op("pe", lambda e: e.matmul(PS[bank][:, 0:n], lhsT=slab[:, kc, :], rhs=rhs_fn(kc),
                                                start=(kc == 0), stop=(kc == nk - 1)),
                      reads=[Rslab] + rhs_res, writes=[R_PS[bank]], sig=(kc == nk - 1))

        R_dd = Res("dd")
        dd_jobs = []
        for g in range(3):
            wb = GROUPS[g][0]
            for b in range(4):
                for r0 in range(0, wb - 8, 256):
                    r1 = min(r0 + 256, wb - 8)
                    dd_jobs.append((kvs[g][b, r0:r1], cache[g][b, 8 + r0:8 + r1]))
        dd_i = 0
        kb.dma("sp", bcol[:], bcol_d, writes=[R_bcol])
        kb.op("dve", lambda e: e.tensor_scalar(bq8[:], bcol[:, 24:36], 0.125, None, op0=ALU.mult),
              reads=[R_bcol], writes=[R_bq8])
        kb.op("pool", lambda e: e.memset(ones[:], 1.0), writes=[R_ones])
        with esA:
            scrb["b"] = sb("scrbb", [128, 8, 128], BF16, esA)
            scrb["Rf"], scrb["Rb"] = Res("scrf"), Res("scrb")
            R_S = [[Res(f"S{a}{p}") for p in range(2)] for a in range(2)]
            qT = sb("qT", [128, 4096], BF16, esA)
            kT = sb("kT", [128, 4096], BF16, esA)
            R_qT = [Res(f"qT{i}") for i in range(8)]
            R_kT = [Res(f"kT{i}") for i in range(8)]
            Vb = sb("Vb", [128, 32, 128], BF16, esA)
            R_V = [Res(f"V{i}") for i in range(8)]
            ACC = sb("ACC", [128, 2, 4096], F32, esA)
            R_ACC = [Res(f"ACC{i}") for i in range(8)]
            bP = sb("bP", [128, 3, 2, 2, 128], F32, esA)
            R_bP = Res("bP")
            bS = sb("bS", [128, 21, 2, 4, 8], F32, esA)
            R_bS = Res("bS")
            bN = sb("bN", [8, 3, 2, 4, 8], F32, esA)
            R_bN = Res("bN")
            bvk = [sb(f"bvk{i}", [128, 5, 128], F32, esA) for i in range(2)]
            R_bvk = [Res(f"bvk{i}") for i in range(2)]
            sbt = [sb(f"sbt{i}", [128, 2, 2, 128], F32, esA) for i in range(2)]
            R_sbt = [Res(f"sbt{i}") for i in range(2)]
            PT = [sb(f"PT{i}", [128, 2, 2, 128], BF16, esA) for i in range(2)]
            R_PT = [Res(f"PT{i}") for i in range(2)]
            kvst = [sb(f"kvst{i}", [128, 2, 128], F32, esA) for i in range(2)]
            R_kvst = [Res(f"kvst{i}") for i in range(2)]
            ftmp = [sb(f"ftmp{i}", [128, 512], F32, esA) for i in range(3)]
            R_ftmp = [Res(f"ftmp{i}") for i in range(3)]
            abst = [sb(f"abst{i}", [128, 512], BF16, esA) for i in range(2)]
            R_abst = [Res(f"abst{i}") for i in range(2)]
            qTs = sb("qTs", [128, 3, 32], BF16, esA)
            kTs = sb("kTs", [128, 3, 32], BF16, esA)
            R_qTs = [Res(f"qTs{g}") for g in range(3)]
            R_kTs = [Res(f"kTs{g}") for g in range(3)]
            kvnew = sb("kvnew", [8, 4, 2, 128], F32, esA)
            R_kvnew = Res("kvnew")
            Vnew = sb("Vnew", [8, 4, 128], BF16, esA)
            R_Vnew = Res("Vnew")
            ckb = [sb(f"ckb{i}", [128, 4, 128], BF16, esA) for i in range(2)]
            cvb = [sb(f"cvb{i}", [128, 4, 128], BF16, esA) for i in range(2)]
            R_ckb = [Res(f"ckb{i}") for i in range(2)]
            R_cvb = [Res(f"cvb{i}") for i in range(2)]
            sbs = sb("sbs", [128, 2, 4, 8], F32, esA)
            R_sbs = Res("sbs")
            PTs = sb("PTs", [128, 2, 4, 8], BF16, esA)
            R_PTs = Res("PTs")
            ACCs = sb("ACCs", [128, 2, 32], F32, esA)
            R_ACCs = Res("ACCs")
            fts = sb("fts", [128, 2, 32], F32, esA)
            R_fts = Res("fts")
            absst = sb("absst", [128, 32], BF16, esA)
            R_absst = Res("absst")

            items = [(hp, g) for hp in range(4) for g in range(3)]

            def load_item(ii, defer=False):
                hp, g = items[ii]
                sq, sk, sv = 24 + 4 * g + hp, 36 + 4 * g + hp, 48 + 4 * g + hp
                bi = ii % 2
                kb.dma("sp", bvk[bi][:, 0, :], brep_d[:, 4 * g + hp, 0, :], writes=[R_bvk[bi]])
                kb.dma("sp", bvk[bi][:, 1:5, :], brep_d[:, 12 + 4 * g + hp], writes=[R_bvk[bi]])
                if not defer:
                    return [slab_f32(WIN[s_]) for s_ in (sq, sk, sv)]
                (ra, da, ca), (rb_, db, cb_), (rc_, dc_, cc_) = [slab_f32(WIN[s_], True) for s_ in (sq, sk, sv)]
                da()
                db()
                deferred.append(lambda: None)
                deferred.append(lambda: (ca(), dc_()))
                deferred.append(lambda: None)
                deferred.append(cb_)
                deferred.append(cc_)
                return [ra, rb_, rc_]

            nxt = load_item(0)
            kb.dma("pool", xsTb[:], xsT, writes=[R_xsT])
            for t in range(8):
                kb.dma("pool", xTb[:, :, t * 512:(t + 1) * 512], xT[:, :, t * 512:(t + 1) * 512], writes=[R_xT[t]])
            scr_i = 0
            blk_it = 0
            for ii, (hp, g) in enumerate(items):
                win, d = GROUPS[g]
                nb = 32 // d
                (sq_t, Rsq), (sk_t, Rsk), (sv_t, Rsv) = nxt
                sq, sk = 24 + 4 * g + hp, 36 + 4 * g + hp
                bi = ii % 2
                if g == 0:
                    kb.dma("sp", bP[:], biasP_d[hp], writes=[R_bP])
                    kb.dma("sp", bS[:], biasS_d[hp], writes=[R_bS])
                    kb.dma("sp", bN[:], biasN_d[hp], writes=[R_bN])
                def unit_qk(which, tc):
                    if which == 0:
                        slab, Rs, dst, Rdst, bias_ap, Rb, scale = sq_t, Rsq, qT, R_qT, bq8[:, 4 * g + hp:4 * g + hp + 1], R_bq8, 0.125
                    else:
                        slab, Rs, dst, Rdst, bias_ap, Rb, scale = sk_t, Rsk, kT, R_kT, bcol[:, sk:sk + 1], R_bcol, 1.0
                    bk = pj_bank()
                    proj_fm(bk, 512, slab, Rs, lambda kc: xTb[:, kc, tc * 512:(tc + 1) * 512], [R_xT[tc]])
                    kb.op("act", lambda e: e.activation(out=dst[:, tc * 512:(tc + 1) * 512], in_=PS[bk][:, 0:512],
                                                        func=AF.Identity, bias=bias_ap, scale=scale),
                          reads=[R_PS[bk], Rb], writes=[Rdst[tc]])

                def unit_sqk(which):
                    if which == 0:
                        slab, Rs, dst, Rdst, bias_ap, Rb, scale = sq_t, Rsq, qTs, R_qTs, bq8[:, 4 * g + hp:4 * g + hp + 1], R_bq8, 0.125
                    else:
                        slab, Rs, dst, Rdst, bias_ap, Rb, scale = sk_t, Rsk, kTs, R_kTs, bcol[:, sk:sk + 1], R_bcol, 1.0
                    bk = pj_bank()
                    proj_fm(bk, 32, slab, Rs, lambda kc: xsTb[:, kc, :], [R_xsT])
                    kb.op("act", lambda e: e.activation(out=dst[:, g, :], in_=PS[bk][:, 0:32],
                                                        func=AF.Identity, bias=bias_ap, scale=scale),
                          reads=[R_PS[bk], Rb], writes=[Rdst[g]])

                def unit_v(q4):
                    bk = pj_bank()
                    for j in range(4):
                        blk = q4 * 4 + j
                        n, r = blk // d, blk % d
                        t0 = n * 128 * d + r
                        for kc in range(8):
                            kb.op("pe", lambda e: e.matmul(PS[bk][:, j * 128:(j + 1) * 128],
                                                            lhsT=xTb[:, kc, t0:t0 + 127 * d + 1:d], rhs=sv_t[:, kc, :],
                                                            start=(kc == 0), stop=(kc == 7)),
                                  reads=[Rsv] + [R_xT[c] for c in rngs(t0, t0 + 127 * d + 1)],
                                  writes=[R_PS[bk]], sig=(kc == 7))
                    kb.op("dve", lambda e: e.tensor_tensor(out=Vb[:, q4 * 4:q4 * 4 + 4, :],
                                                           in0=PS[bk][:, 0:512].rearrange("p (j c) -> p j c", j=4),
                                                           in1=bvk[bi][:, 1:5, :], op=ALU.add),
                          reads=[R_PS[bk], R_bvk[bi]], writes=[R_V[q4]])
                    for j in range(4):
                        blk = q4 * 4 + j
                        n, r = blk // d, blk % d
                        if n != nb - 1:
                            continue
                        t0 = n * 128 * d + r
                        ks = cnt.get("kvst", 0) % 2
                        cnt["kvst"] = cnt.get("kvst", 0) + 1
                        bk2 = pj_bank()
                        if bk2 == bk:
                            bk2 = pj_bank()
                        for kc in range(8):
                            kb.op("pe", lambda e: e.matmul(PS[bk2][:, 0:128],
                                                            lhsT=xTb[:, kc, t0:t0 + 127 * d + 1:d], rhs=sk_t[:, kc, :],
                                                            start=(kc == 0), stop=(kc == 7)),
                                  reads=[Rsk] + [R_xT[c] for c in rngs(t0, t0 + 127 * d + 1)],
                                  writes=[R_PS[bk2]], sig=(kc == 7))
                        kb.op("dve", lambda e: e.tensor_tensor(out=kvst[ks][:, 0, :], in0=PS[bk2][:, 0:128],
                                                               in1=bvk[bi][:, 0, :], op=ALU.add),
                              reads=[R_PS[bk2], R_bvk[bi]], writes=[R_kvst[ks]])
                        kb.op("dve", lambda e: e.tensor_tensor(out=kvst[ks][:, 1, :], in0=PS[bk][:, j * 128:(j + 1) * 128],
                                                               in1=bvk[bi][:, 1, :], op=ALU.add),
                              reads=[R_PS[bk], R_bvk[bi]], writes=[R_kvst[ks]])
                        kb.dma("sp", kvp[g][r:r + 127 * d + 1:d, :, hp * 128:(hp + 1) * 128], kvst[ks][:],
                               reads=[R_kvst[ks]], store=True)

                wb = win

                def unit_snew():
                    for b in range(4):
                        bk = pj_bank()
                        for which, slab, Rs in ((0, sk_t, Rsk), (1, sv_t, Rsv)):
                            for kc in range(8):
                                kb.op("pe", lambda e: e.matmul(PS[bk][0:8, which * 128:(which + 1) * 128],
                                                                lhsT=xsTb[:, kc, b * 8:(b + 1) * 8], rhs=slab[:, kc, :],
                                                                start=(kc == 0), stop=(kc == 7)),
                                      reads=[Rs, R_xsT], writes=[R_PS[bk]], sig=(kc == 7))
                        kb.op("dve", lambda e: e.tensor_tensor(out=kvnew[0:8, b], in0=PS[bk][0:8, 0:256].rearrange("p (k c) -> p k c", k=2),
                                                               in1=bvk[bi][0:8, 0:2, :], op=ALU.add),
                              reads=[R_PS[bk], R_bvk[bi]], writes=[R_kvnew])
                    kb.op("pool", lambda e: e.tensor_copy(Vnew[0:8], kvnew[0:8, :, 1, :]), reads=[R_kvnew], writes=[R_Vnew])
                    for b in range(4):
                        kb.dma("sp", kvs[g][b, wb - 8:wb, :, hp * 128:(hp + 1) * 128], kvnew[0:8, b],
                               reads=[R_kvnew], store=True)

                def emit_S(blk, par):
                    n, r = blk // d, blk % d
                    t0 = n * 128 * d + r
                    tq = slice(t0, t0 + 127 * d + 1, d)
                    pcs = (0, 1) if n > 0 else (1,)
                    for ab in range(2):
                        hs = slice(ab * 64, (ab + 1) * 64)
                        for pc in pcs:
                            tk0 = t0 - (1 - pc) * 128 * d
                            tk = slice(tk0, tk0 + 127 * d + 1, d)
                            kb.op("pe", lambda e: e.matmul(PS[ab * 2 + par][:, pc * 128:(pc + 1) * 128],
                                                            lhsT=kT[hs, tk], rhs=qT[hs, tq], start=True, stop=True),
                                  reads=[R_kT[c] for c in rngs(tk0, tk0 + 127 * d + 1)] +
                                        [R_qT[c] for c in rngs(t0, t0 + 127 * d + 1)],
                                  writes=[R_PS[ab * 2 + par]], sig=(pc == 1))

                def emit_B(blk, par):
                    n = blk // d
                    pcs = (0, 1) if n > 0 else (1,)
                    p0 = pcs[0]
                    for ab in range(2):
                        kb.op("dve", lambda e: e.tensor_tensor(
                            out=sbt[par][:, ab, p0:2, :],
                            in0=PS[ab * 2 + par][:, p0 * 128:256].rearrange("p (a c) -> p a c", c=128),
                            in1=bP[:, g, ab, p0:2, :], op=ALU.add),
                              reads=[R_PS[ab * 2 + par], R_bP], writes=[R_sbt[par]])
                    kb.op("act", lambda e: e.activation(out=PT[par][:, :, p0:2, :], in_=sbt[par][:, :, p0:2, :], func=AF.Exp),
                          reads=[R_sbt[par]], writes=[R_PT[par]])

                def emit_C(blk, par):
                    n, r = blk // d, blk % d
                    t0 = n * 128 * d + r
                    tq = slice(t0, t0 + 127 * d + 1, d)
                    pcs = (0, 1) if n > 0 else (1,)
                    p0 = pcs[0]
                    ob = 4 + par
                    for od in range(2):
                        for ab in range(2):
                            hs = slice(ab * 64, (ab + 1) * 64)
                            for pc in pcs:
                                vblk = blk - (1 - pc) * d
                                lhs = Vb[:, vblk, hs] if od == 0 else ones[:, 0:64]
                                rr = [R_PT[par], R_ones] + ([R_V[vblk // 4]] if od == 0 else [])
                                kb.op("pe", lambda e: e.matmul(PS[ob][hs, od * 128:(od + 1) * 128], lhsT=lhs,
                                                                rhs=PT[par][:, ab, pc, :], start=(pc == p0), stop=(pc == 1)),
                                      reads=rr, writes=[R_PS[ob]], sig=(od == 1 and ab == 1 and pc == 1))
                    cs = rngs(t0, t0 + 127 * d + 1)
                    src = PS[ob][:, 0:256].rearrange("p (a c) -> p a c", a=2)
                    if g == 0:
                        kb.op("act", lambda e: e.copy(ACC[:, :, tq], src),
                              reads=[R_PS[ob]], writes=[R_ACC[c] for c in cs])
                    else:
                        kb.op("dve", lambda e: e.tensor_tensor(out=ACC[:, :, tq], in0=src, in1=ACC[:, :, tq], op=ALU.add),
                              reads=[R_PS[ob]] + [R_ACC[c] for c in cs], writes=[R_ACC[c] for c in cs])

                def s_dma(rc):
                    cb = rc % 2
                    kb.dma("pool", ckb[cb][:], ckT[g][:, hp * 128:(hp + 1) * 128, rc * 128:(rc + 1) * 128].rearrange("b p r -> p b r"),
                           writes=[R_ckb[cb]])
                    kb.dma("pool", cvb[cb][:], cache[g][:, rc * 128:(rc + 1) * 128, 1, hp * 128:(hp + 1) * 128].rearrange("b p c -> p b c"),
                           writes=[R_cvb[cb]])

                def s_cast(rc):
                    pass

                def s_S(rc, par):
                    for ab in range(2):
                        hs = slice(ab * 64, (ab + 1) * 64)
                        for b in range(4):
                            if rc >= 0:
                                cb = rc % 2
                                kb.op("pe", lambda e: e.matmul(PS[ab * 2 + par][:, b * 8:(b + 1) * 8], lhsT=ckb[cb][hs, b, :],
                                                                rhs=qTs[hs, g, b * 8:(b + 1) * 8], start=True, stop=True),
                                      reads=[R_ckb[cb], R_qTs[g]], writes=[R_PS[ab * 2 + par]], sig=(b == 3))
                            else:
                                kb.op("pe", lambda e: e.matmul(PS[ab * 2 + par][0:8, b * 8:(b + 1) * 8], lhsT=kTs[hs, g, b * 8:(b + 1) * 8],
                                                                rhs=qTs[hs, g, b * 8:(b + 1) * 8], start=True, stop=True),
                                      reads=[R_kTs[g], R_qTs[g]], writes=[R_PS[ab * 2 + par]], sig=(b == 3))

                def s_B(rc, par):
                    for ab in range(2):
                        if rc >= 0:
                            kb.op("dve", lambda e: e.tensor_tensor(out=sbs[:, ab], in0=PS[ab * 2 + par][:, 0:32].rearrange("p (b t) -> p b t", b=4),
                                                                   in1=bS[:, CH0[g] + rc, ab], op=ALU.add),
                                  reads=[R_PS[ab * 2 + par], R_bS], writes=[R_sbs])
                        else:
                            kb.op("dve", lambda e: e.tensor_tensor(out=sbs[0:8, ab], in0=PS[ab * 2 + par][0:8, 0:32].rearrange("p (b t) -> p b t", b=4),
                                                                   in1=bN[0:8, g, ab], op=ALU.add),
                                  reads=[R_PS[ab * 2 + par], R_bN], writes=[R_sbs])
                    if rc >= 0:
                        kb.op("act", lambda e: e.activation(out=PTs[:], in_=sbs[:], func=AF.Exp), reads=[R_sbs], writes=[R_PTs])
                    else:
                        kb.op("act", lambda e: e.activation(out=PTs[0:8], in_=sbs[0:8], func=AF.Exp), reads=[R_sbs], writes=[R_PTs])

                def s_C(rc, par):
                    ob = 4 + par
                    for od in range(2):
                        for ab in range(2):
                            hs = slice(ab * 64, (ab + 1) * 64)
                            for b in range(4):
                                if rc >= 0:
                                    cb = rc % 2
                                    lhs = cvb[cb][:, b, hs] if od == 0 else ones[:, 0:64]
                                    kb.op("pe", lambda e: e.matmul(PS[ob][hs, od * 32 + b * 8: od * 32 + (b + 1) * 8], lhsT=lhs,
                                                                    rhs=PTs[:, ab, b, :], start=True, stop=True),
                                          reads=[R_PTs, R_cvb[cb], R_ones], writes=[R_PS[ob]],
                                          sig=(od == 1 and ab == 1 and b == 3))
                                else:
                                    lhs = Vnew[0:8, b, hs] if od == 0 else ones[0:8, 0:64]
                                    kb.op("pe", lambda e: e.matmul(PS[ob][hs, od * 32 + b * 8: od * 32 + (b + 1) * 8], lhsT=lhs,
                                                                    rhs=PTs[0:8, ab, b, :], start=True, stop=True),
                                          reads=[R_PTs, R_Vnew, R_ones], writes=[R_PS[ob]],
                                          sig=(od == 1 and ab == 1 and b == 3))
                    src = PS[ob][:, 0:64].rearrange("p (a c) -> p a c", a=2)
                    if g == 0 and rc == 0:
                        kb.op("dve", lambda e: e.tensor_copy(ACCs[:], src), reads=[R_PS[ob]], writes=[R_ACCs])
                    else:
                        kb.op("dve", lambda e: e.tensor_tensor(out=ACCs[:], in0=src, in1=ACCs[:], op=ALU.add),
                              reads=[R_PS[ob], R_ACCs], writes=[R_ACCs])

                def st_S(ent):
                    (emit_S if ent[0] == "p" else s_S)(ent[1], ent[2])

                def st_B(ent):
                    (emit_B if ent[0] == "p" else s_B)(ent[1], ent[2])

                def st_C(ent):
                    (emit_C if ent[0] == "p" else s_C)(ent[1], ent[2])

                pipe = []

                def push(kind, idx):
                    nonlocal blk_it, scr_i, dd_i
                    par = blk_it % 2
                    blk_it += 1
                    ent = (kind, idx, par)
                    st_S(ent)
                    if len(pipe) >= 1:
                        st_B(pipe[-1])
                    if len(pipe) >= 2:
                        st_C(pipe[-2])
                        pipe.pop(0)
                    pipe.append(ent)
                    if kind != "p":
                        return
                    blk = idx
                    if blk % 6 == 5 and scr_i < len(scr_jobs):
                        jdx, src, nk = scr_jobs[scr_i]
                        scr_i += 1
                        to_scratch(jdx, src[:, 0:nk, :], nk)
                    if blk % 8 == 3 and dd_i < len(dd_jobs):
                        kb.dma("sp", dd_jobs[dd_i][0], dd_jobs[dd_i][1], reads=[R_dd], store=True)
                        dd_i += 1

                def flush():
                    if len(pipe) == 2:
                        st_B(pipe[1])
                        st_C(pipe[0])
                        st_C(pipe[1])
                    elif len(pipe) == 1:
                        st_B(pipe[0])
                        st_C(pipe[0])
                    pipe.clear()

                units = [(-1, lambda: unit_sqk(0)), (-1, lambda: unit_sqk(1)), (-1, unit_snew)]
                for tc in range(8):
                    units.append((tc, lambda tc=tc: unit_qk(0, tc)))
                    units.append((tc, lambda tc=tc: unit_qk(1, tc)))
                    units.append((tc, lambda tc=tc: unit_v(tc)))
                if g < 2:
                    ready_after = {tc: list(range(4 * tc, 4 * tc + 4)) for tc in range(8)}
                else:
                    ready_after = {3: list(range(0, 16)), 7: list(range(16, 32))}
                ready = []
                ui = 0
                prefetched = False
                s_list = list(range(NCH[g])) + [-1]
                s_every = max(1, 32 // len(s_list))
                s_next = 0
                s_dma(0)

                def emit_unit():
                    nonlocal ui, nxt, prefetched
                    tcu, fn = units[ui]
                    fn()
                    ui += 1
                    if ui == len(units) or units[ui][0] != tcu:
                        if tcu in ready_after:
                            ready.extend(ready_after[tcu])
                    if ui == len(units) and not prefetched:
                        prefetched = True
                        nxt = load_item(ii + 1, True) if ii + 1 < len(items) else None

                def push_sample():
                    nonlocal s_next
                    rc = s_list[s_next]
                    s_next += 1
                    if rc >= 0:
                        s_cast(rc)
                    push("s", rc)
                    if s_next < len(s_list) and s_list[s_next] >= 0:
                        s_dma(s_list[s_next])

                nblk = 0
                while ui < len(units) or ready:
                    if not ready:
                        emit_unit()
                        continue
                    push("p", ready.pop(0))
                    run_deferred()
                    nblk += 1
                    if nblk % s_every == 0 and s_next < len(s_list) and ui >= 3:
                        push_sample()
                    if ui < len(units):
                        emit_unit()
                while s_next < len(s_list):
                    push_sample()
                    run_deferred()
                flush()
                run_deferred(len(deferred))

                if g == 2:
                    zs = 60 + hp
                    zb_t, Rzb = slab_f32(WIN[zs])
                    kb.op("act", lambda e: e.activation(out=ACC[:, 1, :], in_=ACC[:, 1, :], func=AF.Ln), reads=R_ACC, writes=R_ACC)
                    kb.op("act", lambda e: e.activation(out=ACC[:, 1, :], in_=ACC[:, 1, :], func=AF.Exp, scale=-1.0),
                          reads=R_ACC, writes=R_ACC)
                    for tc in range(8):
                        fb = tc % 2
                        bk = pj_bank()
                        proj_fm(bk, 512, zb_t, Rzb, lambda kc: xTb[:, kc, tc * 512:(tc + 1) * 512], [R_xT[tc]])
                        kb.op("act", lambda e: e.activation(out=ftmp[2][:], in_=PS[bk][:, 0:512], func=AF.Silu,
                                                            bias=bcol[:, zs:zs + 1], scale=1.0),
                              reads=[R_PS[bk], R_bcol], writes=[R_ftmp[2]])
                        sl = slice(tc * 512, (tc + 1) * 512)
                        kb.op("dve", lambda e: e.tensor_tensor(out=ftmp[fb][:], in0=ACC[:, 0, sl], in1=ACC[:, 1, sl], op=ALU.mult),
                              reads=[R_ACC[tc]], writes=[R_ftmp[fb]])
                        kb.op("dve", lambda e: e.tensor_tensor(out=abst[fb][:], in0=ftmp[fb][:], in1=ftmp[2][:], op=ALU.mult),
                              reads=[R_ftmp[fb], R_ftmp[2]], writes=[R_abst[fb]])
                        kb.dma("sp", ABS[:, hp, sl], abst[fb][:], reads=[R_abst[fb]], writes=[R_abs[hp][tc]], store=True)
                    bk = pj_bank()
                    proj_fm(bk, 32, zb_t, Rzb, lambda kc: xsTb[:, kc, :], [R_xsT])
                    kb.op("act", lambda e: e.activation(out=fts[:, 1, :], in_=PS[bk][:, 0:32], func=AF.Silu,
                                                        bias=bcol[:, zs:zs + 1], scale=1.0),
                          reads=[R_PS[bk], R_bcol], writes=[R_fts])
                    kb.op("dve", lambda e: e.reciprocal(fts[:, 0, :], ACCs[:, 1, :]), reads=[R_ACCs], writes=[R_fts])
                    kb.op("dve", lambda e: e.tensor_tensor(out=fts[:, 0, :], in0=ACCs[:, 0, :], in1=fts[:, 0, :], op=ALU.mult),
                          reads=[R_ACCs, R_fts], writes=[R_fts])
                    kb.op("dve", lambda e: e.tensor_tensor(out=absst[:], in0=fts[:, 0, :], in1=fts[:, 1, :], op=ALU.mult),
                          reads=[R_fts], writes=[R_absst])
                    kb.dma("sp", ABSs[:, hp, :], absst[:], reads=[R_absst], writes=[R_abss[hp]], store=True)

            while dd_i < len(dd_jobs):
                kb.dma("sp", dd_jobs[dd_i][0], dd_jobs[dd_i][1], reads=[R_dd], store=True)
                dd_i += 1
            while scr_i < len(scr_jobs):
                idx, src, nk = scr_jobs[scr_i]
                scr_i += 1
                to_scratch(idx, src[:, 0:nk, :], nk)
            kb.barrier()

        esB = ExitStack()
        with esB:
            Ub = sb("Ub", [128, 8, 542], BF16, esB)
            R_U = [Res(f"U{c}") for c in range(8)]
            Utail = sb("Utail", [128, 8, 30], F32, esB)
            R_Utail = Res("Utail")
            Us = sb("Us", [128, 8, 4, 38], F32, esB)
            Usb = sb("Usb", [128, 8, 4, 38], BF16, esB)
            R_Us = [Res(f"Us{c}") for c in range(8)]
            R_Usb = [Res(f"Usb{c}") for c in range(8)]
            Dg = [sb(f"Dg{i}", [128, 31, 128], BF16, esB) for i in range(2)]
            R_Dg = [[Res(f"Dg{i}a"), Res(f"Dg{i}b")] for i in range(2)]
            KS = 16
            identb = sb("identb", [128, 128], BF16, esB)
            R_ident = Res("ident")
            acc = sb("acc", [128, 8, 512], F32, esB)
            R_acc = [Res(f"acc{c}") for c in range(8)]
            sig_t = [sb(f"sig{i}", [128, 512], F32, esB) for i in range(2)]
            R_sig = [Res(f"sig{i}") for i in range(2)]
            vb_t = [sb(f"vb{i}", [128, 2, 512], BF16, esB) for i in range(2)]
            R_vb = [Res(f"vb{i}") for i in range(2)]
            lnt = sb("lnt", [128, 2, 512], F32, esB)
            R_lnt = Res("lnt")
            sz_t = [sb(f"sz{i}", [128, 512], F32, esB) for i in range(2)]
            R_sz = [Res(f"sz{i}") for i in range(2)]
            caT = sb("caT", [128, 8, 512], BF16, esB)
            R_caT = [Res(f"caT{c}") for c in range(8)]
            mT = sb("mT", [128, 8, 512], BF16, esB)
            R_mT = [Res(f"mT{c}") for c in range(8)]
            abt = [sb(f"abt{i}", [128, 4, 512], BF16, esB) for i in range(1)]
            R_abt = [Res(f"abt{i}") for i in range(1)]
            woutb = sb("woutb", [128, 8, 1024], BF16, esB)
            R_wout = Res("wout")
            lngb = sb("lngb", [128, 2, 1024], F32, esB)
            R_lngb = Res("lngb")
            convw = sb("convw", [128, 8, 31], F32, esB)
            cvec = sb("cvec", [128, 3, 8], F32, esB)
            R_cw = Res("cw")
            xt = [sb(f"xt{i}", [128, 1024], F32, esB) for i in range(1)]
            R_xt = [Res(f"xt{i}") for i in range(1)]
            rb = [sb(f"rb{i}", [128, 1024], F32, esB) for i in range(2)]
            R_rb = [Res(f"rb{i}") for i in range(2)]
            negh = sb("negh", [128, 1], F32, esB)
            R_negh = Res("negh")
            kb.op("pool", lambda e: e.memset(negh[:], -0.5), writes=[R_negh])
            st = sb("st", [128, 4, 8], F32, esB)
            R_st = [Res(f"st{i}") for i in range(4)]

            kb.dma("sp", convw[:], convw_d, writes=[R_cw])
            kb.dma("sp", cvec[:], cvec_d, writes=[R_cw])
            kb.dma("sp", lngb[:], lngb_d, writes=[R_lngb])
            for c in range(8):
                kb.dma("sp", Us[:, c, :, 0:30], scT[:, c], writes=[R_Us[c]])
            kb.op("pool", lambda e: e.memset(Ub[:, :, 0:30], 0.0), writes=R_U)
            kb.dma("pool", identb[:], ident_d, writes=[R_ident])
            for dc in range(8):
                kb.dma("pool", woutb[:, dc, :], WO[:, dc, :], writes=[R_wout])
            wo_steps = []
            pend = []

            def s4_x(M, j, xsrc, ydst):
                kb.dma("sp", xt[0][0:M, :], xsrc, writes=[R_xt[0]])
                rs = cnt.get("rs", 0) % 2
                cnt["rs"] = cnt.get("rs", 0) + 1
                return rs

            def s4_o(M, j, xsrc, ydst, rs, half):
                bo = 5
                for dc in range(8):
                    kb.op("pe", lambda e: e.matmul(PS[bo][0:M, 0:512], lhsT=mT[:, dc, j * 128: j * 128 + M],
                                                    rhs=woutb[:, dc, half * 512:(half + 1) * 512],
                                                    start=(dc == 0), stop=(dc == 7)),
                          reads=[R_mT[dc], R_wout], writes=[R_PS[bo]], sig=(dc == 7))

            def s4_r(M, j, xsrc, ydst, rs, half):
                bo = 5
                rv = rb[rs][0:M, :]
                kb.op("dve", lambda e: e.scalar_tensor_tensor(out=rv[:, half * 512:(half + 1) * 512],
                                                              in0=xt[0][0:M, half * 512:(half + 1) * 512], scalar=ALPHA,
                                                              in1=PS[bo][0:M, 0:512], op0=ALU.mult, op1=ALU.add),
                      reads=[R_xt[0], R_PS[bo]], writes=[R_rb[rs]])

            def s4_stats(M, j, xsrc, ydst, rs):
                rv = rb[rs][0:M, :]
                Rr = [R_rb[rs]]
                Rst = R_st[rs]
                kb.op("act", lambda e: e.activation(out=xt[0][0:M, :], in_=rv, func=AF.Square, accum_out=st[0:M, rs, 1:2]),
                      reads=Rr, writes=[R_xt[0], Rst])
                kb.op("act", lambda e: e.activation(out=rv, in_=rv, func=AF.Identity, accum_out=st[0:M, rs, 0:1]),
                      reads=Rr, writes=Rr + [Rst])

            def s4_fin(M, j, xsrc, ydst, rs):
                rv = rb[rs][0:M, :]
                Rr = [R_rb[rs]]
                Rst = R_st[rs]
                s_mean, s_tmp, s_rstd, s_nmr = (st[0:M, rs, i:i + 1] for i in range(2, 6))
                kb.op("dve", lambda e: e.tensor_scalar(s_mean, st[0:M, rs, 0:1], 1.0 / 1024, None, op0=ALU.mult), reads=[Rst], writes=[Rst])
                kb.op("dve", lambda e: e.tensor_tensor(out=s_tmp, in0=s_mean, in1=s_mean, op=ALU.mult), reads=[Rst], writes=[Rst])
                kb.op("dve", lambda e: e.scalar_tensor_tensor(out=s_tmp, in0=st[0:M, rs, 1:2], scalar=1.0 / 1024, in1=s_tmp,
                                                              op0=ALU.mult, op1=ALU.subtract), reads=[Rst], writes=[Rst])
                kb.op("dve", lambda e: e.tensor_scalar(s_tmp, s_tmp, LN_EPS, None, op0=ALU.add), reads=[Rst], writes=[Rst])
                kb.op("pool", lambda e: e.tensor_tensor(out=s_rstd, in0=s_tmp, in1=negh[0:M, :], op=ALU.pow),
                      reads=[Rst, R_negh], writes=[Rst])
                kb.op("dve", lambda e: e.scalar_tensor_tensor(out=s_nmr, in0=s_mean, scalar=-1.0, in1=s_rstd,
                                                              op0=ALU.mult, op1=ALU.mult), reads=[Rst], writes=[Rst])
                kb.op("dve", lambda e: e.tensor_scalar(rv, rv, s_rstd, s_nmr, op0=ALU.mult, op1=ALU.add),
                      reads=Rr + [Rst], writes=Rr)
                kb.op("pool", lambda e: e.tensor_tensor(out=rv, in0=rv, in1=lngb[0:M, 0, :], op=ALU.mult),
                      reads=Rr + [R_lngb], writes=Rr)
                kb.op("pool", lambda e: e.tensor_tensor(out=rv, in0=rv, in1=lngb[0:M, 1, :], op=ALU.add),
                      reads=Rr + [R_lngb], writes=Rr)
                kb.dma("sp", ydst, rv, reads=Rr, store=True)

            def stage4_sub(M, j, xsrc, ydst):
                a = (M, j, xsrc, ydst)
                rs = s4_x(*a)
                for half in range(2):
                    s4_o(*a, rs, half)
                    s4_r(*a, rs, half)
                s4_stats(*a, rs)
                s4_fin(*a, rs)

            pre = {}

            def prefetch(idx, nk=8):
                if idx not in pre:
                    pre[idx] = slab_scr(idx, nk)

            def get_slab(idx, nk=8):
                if idx in pre:
                    return pre.pop(idx)
                return slab_scr(idx, nk)

            tiles = [("p", ti) for ti in range(8)] + [("s", 0)]
            dcnt = [0]
            vg_pre = {}
            z_pre = []
            for kind, ti in tiles:
                if kind == "p":
                    N = 512
                    t0 = ti * 512
                    xr = lambda kc: xTb[:, kc, t0:t0 + 512]
                    xres = [R_xT[ti]]
                    RU = R_U
                    uv = lambda cc, k: Ub[:, cc, k:k + 512]
                    v3 = lambda ap: ap
                else:
                    N = 32
                    xr = lambda kc: xsTb[:, kc, :]
                    xres = [R_xsT]
                    RU = R_Usb
                    uv = lambda cc, k: Usb[:, cc, :, k:k + 8]
                    v3 = lambda ap: ap.rearrange("p (s l) -> p s l", s=4)

                ab_i = 0
                if kind == "p":
                    kb.dma("sp", abt[ab_i][:], ABS[:, :, t0:t0 + 512], reads=[R_abs[h][ti] for h in range(4)], writes=[R_abt[ab_i]])
                else:
                    kb.dma("sp", abt[ab_i][:, :, 0:32], ABSs, reads=R_abss, writes=[R_abt[ab_i]])

                S1, S2 = 6, 7
                st1 = {}

                def VG(cc):
                    st1[cc] = (slab_scr(cc), slab_scr(8 + cc))

                def VGmm(cc):
                    (sv_t, Rv), (sg_t, Rg) = st1[cc]
                    proj_fm(cc % 2, N, sv_t, Rv, xr, xres)
                    proj_fm(2 + cc % 2, N, sg_t, Rg, xr, xres)

                def gen_D(cc):
                    di = dcnt[0] % 2
                    dcnt[0] += 1
                    st1[("d", cc)] = di
                    kb.op("dve", lambda e: e.tensor_tensor(out=Dg[di][:, 0:KS, :],
                                                           in0=identb[:].unsqueeze(1).to_broadcast([128, KS, 128]),
                                                           in1=convw[:, cc, 0:KS].unsqueeze(2).to_broadcast([128, KS, 128]), op=ALU.mult),
                          reads=[R_ident, R_cw], writes=[R_Dg[di][0]])
                    kb.op("pool", lambda e: e.tensor_tensor(out=Dg[di][:, KS:31, :],
                                                            in0=identb[:].unsqueeze(1).to_broadcast([128, 31 - KS, 128]),
                                                            in1=convw[:, cc, KS:31].unsqueeze(2).to_broadcast([128, 31 - KS, 128]), op=ALU.mult),
                          reads=[R_ident, R_cw], writes=[R_Dg[di][1]])

                def stats(cc):
                    si = cc % 2
                    kb.op("pe", lambda e: e.matmul(PS[S1][:, 0:N], lhsT=ones[:], rhs=vb_t[si][:, 0, 0:N],
                                                    start=(cc == 0), stop=(cc == 7)),
                          reads=[R_ones, R_vb[si]], writes=[R_PS[S1]], sig=False)
                    kb.op("pe", lambda e: e.matmul(PS[S2][:, 0:N], lhsT=ones[:], rhs=vb_t[si][:, 1, 0:N],
                                                    start=(cc == 0), stop=(cc == 7)),
                          reads=[R_ones, R_vb[si]], writes=[R_PS[S2]], sig=True)

                if kind == "p":
                    if vg_pre:
                        st1.update(vg_pre)
                        vg_pre.clear()
                    else:
                        VG(0)
                        VG(1)
                    VGmm(0)
                    gen_D(0)
                    cur4 = None
                    fin4 = None
                    for cc in range(8):
                        if cc + 2 < 8:
                            VG(cc + 2)
                        if cc + 1 < 8:
                            VGmm(cc + 1)
                        if cc % 2 == 0 and pend:
                            cur4 = pend.pop(0)
                            cur4 = cur4 + (s4_x(*cur4),)
                            s4_o(*cur4, 0)
                        elif cc % 2 == 1 and cur4 is not None:
                            s4_o(*cur4, 1)
                        bv, bg, cbk = cc % 2, 2 + cc % 2, 4
                        si = cc % 2
                        kb.op("act", lambda e: e.activation(out=sig_t[si][:, 0:N], in_=PS[bg][:, 0:N], func=AF.Sigmoid,
                                                            bias=bcol[:, 8 + cc:9 + cc], scale=1.0),
                              reads=[R_PS[bg], R_bcol], writes=[R_sig[si]])
                        if kind == "p":
                            kb.op("dve", lambda e: e.scalar_tensor_tensor(out=Ub[:, cc, 30:542], in0=PS[bv][:, 0:N], scalar=bcol[:, cc:cc + 1],
                                                                          in1=sig_t[si][:, 0:N], op0=ALU.add, op1=ALU.mult),
                                  reads=[R_PS[bv], R_bcol, R_sig[si]], writes=[R_U[cc]])
                            if ti == 7:
                                kb.op("dve", lambda e: e.scalar_tensor_tensor(out=Utail[:, cc, :], in0=PS[bv][:, 482:512], scalar=bcol[:, cc:cc + 1],
                                                                              in1=sig_t[si][:, 482:512], op0=ALU.add, op1=ALU.mult),
                                      reads=[R_PS[bv], R_bcol, R_sig[si]], writes=[R_Utail])
                        else:
                            kb.op("dve", lambda e: e.scalar_tensor_tensor(out=Us[:, cc, :, 30:38], in0=v3(PS[bv][:, 0:N]), scalar=bcol[:, cc:cc + 1],
                                                                          in1=v3(sig_t[si][:, 0:N]), op0=ALU.add, op1=ALU.mult),
                                  reads=[R_PS[bv], R_bcol, R_sig[si]], writes=[R_Us[cc]])
                            kb.op("dve", lambda e: e.tensor_copy(Usb[:, cc], Us[:, cc]), reads=[R_Us[cc]], writes=[R_Usb[cc]])
                        if cc + 1 < 8:
                            gen_D(cc + 1)
                        di = st1[("d", cc)]
                        for k in range(31):
                            kb.op("pe", lambda e: e.matmul(v3(PS[cbk][:, 0:N]), lhsT=Dg[di][:, k, :], rhs=uv(cc, k),
                                                            start=(k == 0), stop=(k == 30)),
                                  reads=[R_Dg[di][0 if k < KS else 1], RU[cc]], writes=[R_PS[cbk]], sig=(k == 30))
                        cbias = cvec[:, 0, cc:cc + 1]
                        kb.op("act", lambda e: e.activation(out=acc[:, cc, 0:N], in_=PS[cbk][:, 0:N], func=AF.Identity, bias=cbias, scale=1.0),
                              reads=[R_PS[cbk], R_cw], writes=[R_acc[cc]])
                        kb.op("act", lambda e: e.activation(out=vb_t[si][:, 0, 0:N], in_=PS[cbk][:, 0:N], func=AF.Identity, bias=cbias, scale=1.0),
                              reads=[R_PS[cbk], R_cw], writes=[R_vb[si]])
                        kb.op("act", lambda e: e.activation(out=vb_t[si][:, 1, 0:N], in_=PS[cbk][:, 0:N], func=AF.Square, bias=cbias, scale=1.0),
                              reads=[R_PS[cbk], R_cw], writes=[R_vb[si]])
                        if cc >= 1:
                            stats(cc - 1)
                        if fin4 is not None:
                            s4_fin(*fin4)
                            fin4 = None
                        if cur4 is not None:
                            if cc % 2 == 0:
                                s4_r(*cur4, 0)
                            else:
                                s4_r(*cur4, 1)
                                s4_stats(*cur4)
                                fin4 = cur4
                                cur4 = None
                        if cc == 5:
                            prefetch(16)
                        elif cc == 6:
                            prefetch(24)
                            prefetch(32)
                        elif cc == 7:
                            prefetch(17)
                            prefetch(25)
                            prefetch(33)
                    stats(7)
                    if fin4 is not None:
                        s4_fin(*fin4)
                        fin4 = None
                    while wo_steps:
                        for f_ in wo_steps.pop(0):
                            f_()
                    while pend:
                        stage4_sub(*pend.pop(0))
                else:
                    subs4 = [pend.pop(0) for _ in range(len(pend))]
                    st4 = []
                    for a_ in subs4:
                        st4.append([a_, None])
                    steps4 = []
                    nsub = len(st4)
                    for p_ in range(2 * nsub + 2):
                        stp = []
                        if p_ >= 2 and (p_ - 2) % 2 == 0 and (p_ - 2) // 2 < nsub:
                            stp.append(("C", (p_ - 2) // 2))
                        if p_ >= 3 and (p_ - 3) % 2 == 0 and (p_ - 3) // 2 < nsub:
                            stp.append(("D", (p_ - 3) // 2))
                        if p_ >= 1 and (p_ - 1) % 2 == 0 and (p_ - 1) // 2 < nsub:
                            stp.append(("B", (p_ - 1) // 2))
                        if p_ % 2 == 0 and p_ // 2 < nsub:
                            stp.append(("A", p_ // 2))
                        steps4.append(stp)

                    def run_pend():
                        if not steps4:
                            return
                        for kind4, j4 in steps4.pop(0):
                            a_, rs_ = st4[j4]
                            if kind4 == "A":
                                st4[j4][1] = s4_x(*a_)
                                s4_o(*a_, st4[j4][1], 0)
                            elif kind4 == "B":
                                s4_r(*a_, rs_, 0)
                                s4_o(*a_, rs_, 1)
                            elif kind4 == "C":
                                s4_r(*a_, rs_, 1)
                                s4_stats(*a_, rs_)
                            else:
                                s4_fin(*a_, rs_)
                    for half in range(2):
                        ccs = list(range(4 * half, 4 * half + 4))
                        sl = {cc: (slab_scr(cc), slab_scr(8 + cc)) for cc in ccs}
                        for cc in ccs:
                            co = (cc % 4) * 32
                            for (slab_, Rs_), bank in ((sl[cc][0], half), (sl[cc][1], 2 + half)):
                                for kc in range(8):
                                    kb.op("pe", lambda e: e.matmul(PS[bank][:, co:co + 32], lhsT=slab_[:, kc, :], rhs=xsTb[:, kc, :],
                                                                    start=(kc == 0), stop=(kc == 7)),
                                          reads=[Rs_, R_xsT], writes=[R_PS[bank]], sig=(kc == 7))
                        for cc in ccs:
                            co = (cc % 4) * 32
                            kb.op("act", lambda e: e.activation(out=sig_t[0][:, cc * 32:(cc + 1) * 32], in_=PS[2 + half][:, co:co + 32],
                                                                func=AF.Sigmoid, bias=bcol[:, 8 + cc:9 + cc], scale=1.0),
                                  reads=[R_PS[2 + half], R_bcol], writes=[R_sig[0]])
                        for cc in ccs:
                            co = (cc % 4) * 32
                            kb.op("dve", lambda e: e.scalar_tensor_tensor(out=Us[:, cc, :, 30:38], in0=v3(PS[half][:, co:co + 32]),
                                                                          scalar=bcol[:, cc:cc + 1], in1=v3(sig_t[0][:, cc * 32:(cc + 1) * 32]),
                                                                          op0=ALU.add, op1=ALU.mult),
                                  reads=[R_PS[half], R_bcol, R_sig[0]], writes=[R_Us[cc]])
                        run_pend()
                    accv = acc[:, :, 0:32].rearrange("p c (s l) -> p c s l", s=4)
                    prodv = sig_t[1][:, 0:256].rearrange("p (c s l) -> p c s l", c=8, s=4)
                    wbc = lambda k: convw[:, :, k:k + 1].unsqueeze(3).to_broadcast([128, 8, 4, 8])
                    kb.op("dve", lambda e: e.tensor_tensor(out=accv, in0=Us[:, :, :, 0:8], in1=wbc(0), op=ALU.mult),
                          reads=R_Us + [R_cw], writes=R_acc)
                    for k in range(1, 31):
                        kb.op("dve", lambda e: e.tensor_tensor(out=prodv, in0=Us[:, :, :, k:k + 8], in1=wbc(k), op=ALU.mult),
                              reads=R_Us + [R_cw], writes=[R_sig[1]])
                        kb.op("dve", lambda e: e.tensor_tensor(out=accv, in0=accv, in1=prodv, op=ALU.add),
                              reads=R_acc + [R_sig[1]], writes=R_acc)
                        if k % 4 == 0:
                            run_pend()
                    kb.op("dve", lambda e: e.tensor_tensor(out=accv, in0=accv,
                                                           in1=cvec[:, 0, :].unsqueeze(2).unsqueeze(3).to_broadcast([128, 8, 4, 8]), op=ALU.add),
                          reads=R_acc + [R_cw], writes=R_acc)
                    vview = lambda i: vb_t[0][:, i, 0:256].rearrange("p (c n) -> p c n", c=8)
                    kb.op("act", lambda e: e.activation(out=vview(0), in_=acc[:, :, 0:32], func=AF.Copy), reads=R_acc, writes=[R_vb[0]])
                    kb.op("act", lambda e: e.activation(out=vview(1), in_=acc[:, :, 0:32], func=AF.Square), reads=R_acc, writes=[R_vb[0]])
                    for cc in range(8):
                        kb.op("pe", lambda e: e.matmul(PS[S1][:, 0:32], lhsT=ones[:], rhs=vb_t[0][:, 0, cc * 32:(cc + 1) * 32],
                                                        start=(cc == 0), stop=(cc == 7)),
                              reads=[R_ones, R_vb[0]], writes=[R_PS[S1]], sig=(cc == 7))
                    for cc in range(8):
                        kb.op("pe", lambda e: e.matmul(PS[S2][:, 0:32], lhsT=ones[:], rhs=vb_t[0][:, 1, cc * 32:(cc + 1) * 32],
                                                        start=(cc == 0), stop=(cc == 7)),
                              reads=[R_ones, R_vb[0]], writes=[R_PS[S2]], sig=(cc == 7))
                    while steps4:
                        run_pend()
                if kind == "p":
                    if ti == 7:
                        for c in range(0, 8, 2):
                            kb.dma("sp", convp[:, c:c + 2], Utail[:, c:c + 2, :], reads=[R_Utail], store=True)
                    else:
                        kb.op("pool", lambda e: e.tensor_copy(Ub[:, :, 0:30], Ub[:, :, 512:542]), reads=R_U, writes=R_U)
                else:
                    for c in range(8):
                        kb.dma("sp", convs[:, c], Us[:, c, :, 8:38], reads=[R_Us[c]], store=True)
                mean, tmp = sig_t[0][:, 0:N], sig_t[1][:, 0:N]
                Rmt = [R_sig[0], R_sig[1]]
                rstd, nmr = lnt[:, 0, 0:N], lnt[:, 1, 0:N]
                kb.op("act", lambda e: e.mul(mean, PS[S1][:, 0:N], 1.0 / 1024), reads=[R_PS[S1]], writes=[R_sig[0]])
                kb.op("dve", lambda e: e.tensor_tensor(out=tmp, in0=mean, in1=mean, op=ALU.mult), reads=[R_sig[0]], writes=[R_sig[1]])
                kb.op("dve", lambda e: e.scalar_tensor_tensor(out=tmp, in0=PS[S2][:, 0:N], scalar=1.0 / 1024, in1=tmp,
                                                              op0=ALU.mult, op1=ALU.subtract),
                      reads=[R_PS[S2], R_sig[1]], writes=[R_sig[1]])
                kb.op("dve", lambda e: e.tensor_scalar(tmp, tmp, LN_EPS, None, op0=ALU.add), reads=[R_sig[1]], writes=[R_sig[1]])
                kb.op("act", lambda e: e.sqrt(tmp, tmp), reads=[R_sig[1]], writes=[R_sig[1]])
                kb.op("dve", lambda e: e.reciprocal(rstd, tmp), reads=[R_sig[1]], writes=[R_lnt])
                kb.op("dve", lambda e: e.scalar_tensor_tensor(out=nmr, in0=mean, scalar=-1.0, in1=rstd, op0=ALU.mult, op1=ALU.mult),
                      reads=[R_sig[0], R_lnt], writes=[R_lnt])

                def gg_proj(dc):
                    ga_t, Rga = get_slab(24 + dc)
                    gb_t, Rgb = get_slab(32 + dc)
                    proj_fm(4 + dc % 2, N, ga_t, Rga, xr, xres)
                    proj_fm(6 + dc % 2, N, gb_t, Rgb, xr, xres)

                for cc in range(8):
                    if cc == 2:
                        gg_proj(0)
                        gg_proj(1)
                    sz_s, Rz = get_slab(16 + cc)
                    if cc + 1 < 8:
                        prefetch(16 + cc + 1)
                    if cc == 4:
                        prefetch(40)
                        prefetch(48, 4)
                    bz = cc % 2
                    si = cc % 2
                    proj_fm(bz, N, sz_s, Rz, xr, xres)
                    kb.op("act", lambda e: e.activation(out=sz_t[si][:, 0:N], in_=PS[bz][:, 0:N], func=AF.Silu,
                                                        bias=bcol[:, 16 + cc:17 + cc], scale=1.0),
                          reads=[R_PS[bz], R_bcol], writes=[R_sz[si]])
                    a = acc[:, cc, 0:N]
                    gsc = cvec[:, 1, cc:cc + 1]
                    kb.op("dve", lambda e: e.scalar_tensor_tensor(out=a, in0=a, scalar=gsc, in1=rstd, op0=ALU.mult, op1=ALU.mult),
                          reads=[R_acc[cc], R_lnt, R_cw], writes=[R_acc[cc]])
                    kb.op("dve", lambda e: e.scalar_tensor_tensor(out=a, in0=nmr, scalar=gsc, in1=a, op0=ALU.mult, op1=ALU.add),
                          reads=[R_acc[cc], R_lnt, R_cw], writes=[R_acc[cc]])
                    kb.op("act", lambda e: e.activation(out=a, in_=a, func=AF.Silu, bias=cvec[:, 2, cc:cc + 1], scale=1.0),
                          reads=[R_acc[cc], R_cw], writes=[R_acc[cc]])
                    kb.op("dve", lambda e: e.tensor_tensor(out=caT[:, cc, 0:N], in0=a, in1=sz_t[si][:, 0:N], op=ALU.mult),
                          reads=[R_acc[cc], R_sz[si]], writes=[R_caT[cc]])

                for dc in range(8):
                    wa_t, Rwa = get_slab(40 + dc)
                    wb_t, Rwb = get_slab(48 + dc, 4)
                    if dc + 1 < 8:
                        prefetch(40 + dc + 1)
                        prefetch(48 + dc + 1, 4)
                    bpa, bpb, bga, bgb = dc % 2, 2 + dc % 2, 4 + dc % 2, 6 + dc % 2
                    for cc in range(8):
                        kb.op("pe", lambda e: e.matmul(PS[bpa][:, 0:N], lhsT=wa_t[:, cc, :], rhs=caT[:, cc, 0:N],
                                                        start=(cc == 0), stop=(cc == 7)),
                              reads=[Rwa, R_caT[cc]], writes=[R_PS[bpa]], sig=(cc == 7))
                    for fc in range(4):
                        kb.op("pe", lambda e: e.matmul(PS[bpb][:, 0:N], lhsT=wb_t[:, fc, :], rhs=abt[ab_i][:, fc, 0:N],
                                                        start=(fc == 0), stop=(fc == 3)),
                              reads=[Rwb, R_abt[ab_i]], writes=[R_PS[bpb]], sig=(fc == 3))
                    o3 = 2 * (dc % 2)
                    t_sa, t_sb = acc[:, o3, 0:N], acc[:, o3 + 1, 0:N]
                    Rsa, Rsb = R_acc[o3], R_acc[o3 + 1]
                    kb.op("act", lambda e: e.activation(out=t_sa, in_=PS[bga][:, 0:N], func=AF.Sigmoid,
                                                        bias=bcol[:, 64 + dc:65 + dc], scale=1.0),
                          reads=[R_PS[bga], R_bcol], writes=[Rsa])
                    kb.op("act", lambda e: e.activation(out=t_sb, in_=PS[bgb][:, 0:N], func=AF.Sigmoid,
                                                        bias=bcol[:, 72 + dc:73 + dc], scale=1.0),
                          reads=[R_PS[bgb], R_bcol], writes=[Rsb])
                    if dc + 2 < 8:
                        gg_proj(dc + 2)
                    kb.op("dve", lambda e: e.tensor_tensor(out=t_sa, in0=PS[bpa][:, 0:N], in1=t_sa, op=ALU.mult),
                          reads=[R_PS[bpa], Rsa], writes=[Rsa])
                    kb.op("dve", lambda e: e.tensor_tensor(out=t_sb, in0=PS[bpb][:, 0:N], in1=t_sb, op=ALU.mult),
                          reads=[R_PS[bpb], Rsb], writes=[Rsb])
                    kb.op("pool", lambda e: e.tensor_tensor(out=mT[:, dc, 0:N], in0=t_sa, in1=t_sb, op=ALU.add),
                          reads=[Rsa, Rsb], writes=[R_mT[dc]])
                    if dc == 6 and kind == "p" and ti < 7:
                        vg_pre[0] = (slab_scr(0), slab_scr(8))
                    if dc == 7 and kind == "p" and ti < 7:
                        vg_pre[1] = (slab_scr(1), slab_scr(9))

                if kind == "p":
                    for j in range(4):
                        pend.append((128, j, x[t0 + j * 128: t0 + (j + 1) * 128, :], y[t0 + j * 128: t0 + (j + 1) * 128, :]))
                else:
                    pend.append((32, 0, xs, ys))

            while pend:
                stage4_sub(*pend.pop(0))

            kb.finish()
    return nc


_CACHE = {}


def _tables():
    if "t" in _CACHE:
        return _CACHE["t"]
    slopes = 2.0 ** (-8.0 * np.arange(1, 9, dtype=np.float64) / 8)
    jj = np.arange(128)[:, None]
    ii = np.arange(128)[None, :]
    biasP = np.full((4, 128, 3, 2, 2, 128), NEG, np.float32)
    biasS = np.full((4, 128, 21, 2, 4, 8), NEG, np.float32)
    biasN = np.full((4, 8, 3, 2, 4, 8), NEG, np.float32)
    for hp in range(4):
        for ab in range(2):
            m = slopes[2 * hp + ab]
            for g, (win, d) in enumerate(GROUPS):
                dprev = ii + 128 - jj
                biasP[hp, :, g, ab, 0, :] = np.where(dprev <= 128, -m * dprev * d, NEG)
                dcur = ii - jj
                biasP[hp, :, g, ab, 1, :] = np.where(dcur >= 0, -m * dcur * d, NEG)
                wb = win
                t = np.arange(8)[None, :]
                for rc in range(NCH[g]):
                    r = rc * 128 + np.arange(128)[:, None]
                    dist = wb + t - r
                    ok = (dist % d == 0) & (dist // d <= 128)
                    v = np.where(ok, -m * dist, NEG)
                    biasS[hp, :, CH0[g] + rc, ab, :, :] = v[:, None, :]
                s = np.arange(8)[:, None]
                dist = t - s
                ok = (dist >= 0) & (dist % d == 0)
                v = np.where(ok, -m * dist, NEG)
                biasN[hp, :, g, ab, :, :] = v[:, None, :]
    _CACHE["t"] = (biasP, biasS, biasN)
    return _CACHE["t"]


def kernel(x_prompt, x_sample, cache_kv_w128, cache_kv_w512, cache_kv_w2048, state_conv,
           w_in, b_in, conv_w, conv_b, conv_ln_g, conv_ln_b, w_a, w_b, w_out, ln_g, ln_b):
    f = lambda a: np.ascontiguousarray(np.asarray(a, dtype=np.float32))
    x_prompt, x_sample = f(x_prompt), f(x_sample)
    caches = [f(cache_kv_w128), f(cache_kv_w512), f(cache_kv_w2048)]
    state_conv = f(state_conv)
    w_in, b_in, conv_w = f(w_in), f(b_in), f(conv_w)
    biasP, biasS, biasN = _tables()
    shared = {
        "WIN": f(w_in.reshape(8, 128, 80, 128).transpose(2, 1, 0, 3)),
        "WA": f(f(w_a).reshape(8, 128, 8, 128).transpose(2, 1, 0, 3)),
        "WB": f(f(w_b).reshape(4, 128, 8, 128).transpose(2, 1, 0, 3)),
        "WO": f(f(w_out).reshape(8, 128, 1024).transpose(1, 0, 2)),
        "bcol": f(b_in.reshape(80, 128).T),
        "brep": f(np.broadcast_to(b_in[4608:7680].reshape(1, 24, 1, 128), (128, 24, 4, 128))),
        "convw": f(conv_w.reshape(31, 8, 128).transpose(2, 1, 0)),
        "cvec": f(np.stack([f(conv_b).reshape(8, 128).T, f(conv_ln_g).reshape(8, 128).T,
                            f(conv_ln_b).reshape(8, 128).T], axis=1)),
        "lngb": f(np.broadcast_to(np.stack([f(ln_g), f(ln_b)], 0)[None], (128, 2, 1024))),
        "biasP": biasP, "biasS": biasS, "biasN": biasN, "ident": np.eye(128, dtype=np.float32),
    }
    in_maps = []
    for c in range(8):
        m = dict(shared)
        xp = x_prompt[c]
        m["x"] = xp
        m["xT"] = f(xp.T.reshape(8, 128, 4096).transpose(1, 0, 2))
        xs_ = x_sample[4 * c:4 * c + 4].reshape(32, 1024)
        m["xs"] = f(xs_)
        m["xsT"] = f(xs_.T.reshape(8, 128, 32).transpose(1, 0, 2))
        for g in range(3):
            wb = GROUPS[g][0]
            cg = caches[g][4 * c:4 * c + 4].reshape(4, wb, 2, 512)
            m[f"c{g}"] = f(cg)
            m[f"k{g}"] = f(cg[:, :, 0, :].transpose(0, 2, 1))
        sc = state_conv[4 * c:4 * c + 4]
        m["scT"] = f(sc.reshape(4, 30, 8, 128).transpose(3, 2, 0, 1))
        in_maps.append(m)
    if "nc" not in _CACHE:
        _CACHE["nc"] = build_program()
    res = run_bass_kernel_spmd(_CACHE["nc"], in_maps, core_ids=list(range(8)))
    R = res.results
    y_p = np.stack([R[c]["y"] for c in range(8)], 0)
    y_s = np.concatenate([R[c]["ys"].reshape(4, 8, 1024) for c in range(8)], 0)
    kvp = [np.stack([R[c][f"kvp{g}"].reshape(GROUPS[g][0], 2, 8, 64) for c in range(8)], 0) for g in range(3)]
    conv_p = np.stack([R[c]["convp"].transpose(2, 1, 0).reshape(30, 1024) for c in range(8)], 0)
    kvs = [np.concatenate([R[c][f"kvs{g}"].reshape(4, GROUPS[g][0], 2, 8, 64) for c in range(8)], 0) for g in range(3)]
    conv_s = np.concatenate([R[c]["convs"].transpose(2, 3, 1, 0).reshape(4, 30, 1024) for c in range(8)], 0)
    outs = (y_p, y_s, kvp[0], kvp[1], kvp[2], conv_p, kvs[0], kvs[1], kvs[2], conv_s)
    return tuple(np.ascontiguousarray(o, dtype=np.float32) for o in outs)
```

```python
import numpy as np
import concourse.bass as bass
import concourse.mybir as mybir
from concourse.bass_utils import run_bass_kernel_spmd
from contextlib import ExitStack

F32 = mybir.dt.float32
BF16 = mybir.dt.bfloat16
AF = mybir.ActivationFunctionType
ALU = mybir.AluOpType

S = 4096
GROUPS = ((128, 1), (512, 4), (2048, 16))
ALPHA = 2.0 ** 0.25
LN_EPS = 1e-5
NEG = -30000.0
NCH = (1, 4, 16)
CH0 = (0, 1, 5)
KD = 31
SEG = 3000


class Res:
    __slots__ = ("name", "w", "r", "dsem", "dcnt", "ssem", "scnt", "qsem", "qcnt")

    def __init__(self, name):
        self.name = name
        self.w = None
        self.r = {}
        self.dsem = None
        self.dcnt = 0
        self.ssem = None
        self.scnt = 0
        self.qsem = None
        self.qcnt = 0


class KB:
    CE = ("pe", "act", "dve", "pool")

    def __init__(self, nc, es):
        self.nc = nc
        self.es = es
        self.E = {"pe": nc.tensor, "act": nc.scalar, "dve": nc.vector, "pool": nc.gpsimd, "sp": nc.sync}
        self.nsig = {e: 0 for e in self.CE}
        self.sems = {e: [] for e in self.CE}
        self.seen = {e: {} for e in self.E}
        self.nsem = 0
        self.dma_ev = {}

    def newsem(self):
        self.nsem += 1
        return self.es.enter_context(self.nc.semaphore(f"s{self.nsem}"))

    def _sem_for(self, e, n):
        idx = (n - 1) // SEG
        while len(self.sems[e]) <= idx:
            self.sems[e].append(self.newsem())
        return self.sems[e][idx], (n - 1) % SEG + 1

    def _need(self, e, ev):
        kind, key, val = ev
        k = key if kind == "c" else ("d", id(key))
        if self.seen[e].get(k, 0) >= val:
            return
        self.seen[e][k] = val
        if kind == "c":
            sem, v = self._sem_for(key, val)
            self.E[e].wait_ge(sem, v)
        else:
            self.E[e].wait_ge(key, val)

    def _dep(self, e, ev, raw):
        if ev[0] == "c" and ev[1] == e and not raw and e == "pe":
            return
        self._need(e, ev)

    def op(self, e, fn, reads=(), writes=(), sig=True):
        for r in reads:
            if r.w is not None:
                self._dep(e, r.w, True)
        for w in writes:
            if w.w is not None:
                self._dep(e, w.w, False)
            for x in w.r.values():
                self._dep(e, x, False)
        ins = fn(self.E[e])
        if sig:
            self.nsig[e] += 1
            sem, _ = self._sem_for(e, self.nsig[e])
            ins.then_inc(sem, 1)
            ev = ("c", e, self.nsig[e])
        else:
            ev = ("c", e, self.nsig[e] + 1)
        for r in reads:
            r.r[e] = ev
        for w in writes:
            w.w = ev
            w.r = {}

    def dma(self, q, out, in_, reads=(), writes=(), store=False):
        own = reads[0] if store else writes[0]
        if store:
            if own.ssem is None:
                own.ssem = self.newsem()
            sem = own.ssem
            own.scnt += 16
            val = own.scnt
        elif q == "pool":
            if own.qsem is None:
                own.qsem = self.newsem()
            sem = own.qsem
            own.qcnt += 16
            val = own.qcnt
        else:
            if own.dsem is None:
                own.dsem = self.newsem()
            sem = own.dsem
            own.dcnt += 16
            val = own.dcnt
        for r in reads:
            if r.w is not None:
                self._dep(q, r.w, True)
        for w in writes:
            if w.w is not None and not (w.w[0] == "d" and w.w[1] is sem):
                self._dep(q, w.w, False)
            for x in w.r.values():
                self._dep(q, x, False)
        self.E[q].dma_start(out=out, in_=in_).then_inc(sem, 16)
        ev = ("d", sem, val)
        for r in reads:
            r.r[("d", id(sem))] = ev
        for w in writes:
            w.w = ev
            w.r = {}
        self.dma_ev[id(sem)] = ev
        return ev

    def barrier(self):
        for e in self.E:
            for f in self.CE:
                if f != e and self.nsig[f] > 0:
                    self._need(e, ("c", f, self.nsig[f]))
            for ev in self.dma_ev.values():
                self._need(e, ev)

    def finish(self):
        for ev in self.dma_ev.values():
            self._need("sp", ev)


def build_program():
    nc = bass.Bass("TRN2", target_bir_lowering=False)

    def din(name, shape, dt=F32):
        return nc.dram_tensor(name, list(shape), dt, kind="ExternalInput").ap()

    def dout(name, shape):
        return nc.dram_tensor(name, list(shape), F32, kind="ExternalOutput").ap()

    x = din("x", [4096, 1024])
    xT = din("xT", [128, 8, 4096])
    xs = din("xs", [32, 1024])
    xsT = din("xsT", [128, 8, 32])
    cache = [din("c0", [4, 128, 2, 512]), din("c1", [4, 512, 2, 512]), din("c2", [4, 2048, 2, 512])]
    ckT = [din("k0", [4, 512, 128]), din("k1", [4, 512, 512]), din("k2", [4, 512, 2048])]
    scT = din("scT", [128, 8, 4, 30])
    WIN = din("WIN", [80, 128, 8, 128])
    WA = din("WA", [8, 128, 8, 128])
    WB = din("WB", [8, 128, 4, 128])
    WO = din("WO", [128, 8, 1024])
    bcol_d = din("bcol", [128, 80])
    brep_d = din("brep", [128, 24, 4, 128])
    convw_d = din("convw", [128, 8, 31])
    cvec_d = din("cvec", [128, 3, 8])
    lngb_d = din("lngb", [128, 2, 1024])
    biasP_d = din("biasP", [4, 128, 3, 2, 2, 128])
    biasS_d = din("biasS", [4, 128, 21, 2, 4, 8])
    biasN_d = din("biasN", [4, 8, 3, 2, 4, 8])
    ident_d = din("ident", [128, 128])

    y = dout("y", [4096, 1024])
    ys = dout("ys", [32, 1024])
    kvp = [dout("kvp0", [128, 2, 512]), dout("kvp1", [512, 2, 512]), dout("kvp2", [2048, 2, 512])]
    convp = dout("convp", [128, 8, 30])
    kvs = [dout("kvs0", [4, 128, 2, 512]), dout("kvs1", [4, 512, 2, 512]), dout("kvs2", [4, 2048, 2, 512])]
    convs = dout("convs", [128, 8, 4, 30])

    WSC = nc.dram_tensor("wsc", [56, 128, 8, 128], BF16, kind="Internal").ap()
    ABS = nc.dram_tensor("absc", [128, 4, 4096], BF16, kind="Internal").ap()
    ABSs = nc.dram_tensor("abssc", [128, 4, 32], BF16, kind="Internal").ap()

    with ExitStack() as es:
        kb = KB(nc, es)

        def sb(name, shape, dt, st=es):
            return st.enter_context(nc.sbuf_tensor("s_" + name, list(shape), dt))

        PS = [es.enter_context(nc.psum_tensor(f"ps{i}", [128, 512], F32)) for i in range(8)]
        R_PS = [Res(f"ps{i}") for i in range(8)]
        xTb = sb("xTb", [128, 8, 4096], BF16)
        R_xT = [Res(f"xT{i}") for i in range(8)]
        xsTb = sb("xsTb", [128, 8, 32], BF16)
        R_xsT = Res("xsT")
        NSTG, NSLB = 2, 8
        slb = [sb(f"slb{i}", [128, 8, 128], BF16) for i in range(NSLB)]
        R_slb = [Res(f"slb{i}") for i in range(NSLB)]
        bcol = sb("bcol", [128, 80], F32)
        R_bcol = Res("bcol")
        bq8 = sb("bq8", [128, 12], F32)
        R_bq8 = Res("bq8")
        ones = sb("ones", [128, 128], BF16)
        R_ones = Res("ones")
        esA = ExitStack()
        R_wsc = [Res(f"wsc{i}") for i in range(56)]
        R_abs = [[Res(f"abs{h}_{t}") for t in range(8)] for h in range(4)]
        R_abss = [Res(f"abss{h}") for h in range(4)]
        cnt = {"stg": 0, "slb": 0, "pj": 0}

        def rngs(lo, hi):
            return list(range(lo // 512, (hi - 1) // 512 + 1))

        deferred = []

        def slab_f32(src, defer=False):
            j = cnt["slb"] % NSLB
            cnt["slb"] += 1
            kb.dma("pool", slb[j][:], src, writes=[R_slb[j]])
            if defer:
                return (slb[j], R_slb[j]), (lambda: None), (lambda: None)
            return slb[j], R_slb[j]

        def run_deferred(n=1):
            for _ in range(n):
                if deferred:
                    deferred.pop(0)()

        def slab_scr(idx, nk=8):
            j = cnt["slb"] % NSLB
            cnt["slb"] += 1
            kb.dma("sp", slb[j][:, 0:nk, :], WSC[idx, :, 0:nk, :], reads=[R_wsc[idx]], writes=[R_slb[j]])
            return slb[j], R_slb[j]

        scrb = {}

        def to_scratch(idx, src, nk=8):
            kb.dma("pool", scrb["b"][:, 0:nk, :], src, writes=[scrb["Rb"]])
            kb.dma("sp", WSC[idx, :, 0:nk, :], scrb["b"][:, 0:nk, :], reads=[scrb["Rb"]], writes=[R_wsc[idx]], store=True)

        scr_jobs = []
        for c in range(8):
            scr_jobs.append((c, WIN[c], 8))
            scr_jobs.append((8 + c, WIN[8 + c], 8))
            scr_jobs.append((16 + c, WIN[16 + c], 8))
        for c in range(8):
            scr_jobs.append((24 + c, WIN[64 + c], 8))
            scr_jobs.append((32 + c, WIN[72 + c], 8))
            scr_jobs.append((40 + c, WA[c], 8))
            scr_jobs.append((48 + c, WB[c], 4))

        def pj_bank():
            b = 6 + cnt["pj"] % 2
            cnt["pj"] += 1
            return b

        def proj_fm(bank, n, slab, Rslab, rhs_fn, rhs_res, nk=8):
            for kc in range(nk):
                kb.op("pe", lambda e: e.matmul(PS[bank][:, 0:n], lhsT=slab[:, kc, :], rhs=rhs_fn(kc),
                                                start=(kc == 0), stop=(kc == nk - 1)),
                      reads=[Rslab] + rhs_res, writes=[R_PS[bank]], sig=(kc == nk - 1))

        R_dd = Res("dd")
        dd_jobs = []
        for g in range(3):
            wb = GROUPS[g][0]
            for b in range(4):
                for r0 in range(0, wb - 8, 256):
                    r1 = min(r0 + 256, wb - 8)
                    dd_jobs.append((kvs[g][b, r0:r1], cache[g][b, 8 + r0:8 + r1]))
        dd_i = 0
        kb.dma("sp", bcol[:], bcol_d, writes=[R_bcol])
        kb.op("dve", lambda e: e.tensor_scalar(bq8[:], bcol[:, 24:36], 0.125, None, op0=ALU.mult),
              reads=[R_bcol], writes=[R_bq8])
        kb.op("pool", lambda e: e.memset(ones[:], 1.0), writes=[R_ones])
        with esA:
            scrb["b"] = sb("scrbb", [128, 8, 128], BF16, esA)
            scrb["Rf"], scrb["Rb"] = Res("scrf"), Res("scrb")
            R_S = [[Res(f"S{a}{p}") for p in range(2)] for a in range(2)]
            qT = sb("qT", [128, 4096], BF16, esA)
            kT = sb("kT", [128, 4096], BF16, esA)
            R_qT = [Res(f"qT{i}") for i in range(8)]
            R_kT = [Res(f"kT{i}") for i in range(8)]
            Vb = sb("Vb", [128, 32, 128], BF16, esA)
            R_V = [Res(f"V{i}") for i in range(8)]
            ACC = sb("ACC", [128, 2, 4096], F32, esA)
            R_ACC = [Res(f"ACC{i}") for i in range(8)]
            bP = sb("bP", [128, 3, 2, 2, 128], F32, esA)
            R_bP = Res("bP")
            bS = sb("bS", [128, 21, 2, 4, 8], F32, esA)
            R_bS = Res("bS")
            bN = sb("bN", [8, 3, 2, 4, 8], F32, esA)
            R_bN = Res("bN")
            bvk = [sb(f"bvk{i}", [128, 5, 128], F32, esA) for i in range(2)]
            R_bvk = [Res(f"bvk{i}") for i in range(2)]
            sbt = [sb(f"sbt{i}", [128, 2, 2, 128], F32, esA) for i in range(2)]
            R_sbt = [Res(f"sbt{i}") for i in range(2)]
            PT = [sb(f"PT{i}", [128, 2, 2, 128], BF16, esA) for i in range(2)]
            R_PT = [Res(f"PT{i}") for i in range(2)]
            kvst = [sb(f"kvst{i}", [128, 2, 128], F32, esA) for i in range(2)]
            R_kvst = [Res(f"kvst{i}") for i in range(2)]
            ftmp = [sb(f"ftmp{i}", [128, 512], F32, esA) for i in range(3)]
            R_ftmp = [Res(f"ftmp{i}") for i in range(3)]
            abst = [sb(f"abst{i}", [128, 512], BF16, esA) for i in range(2)]
            R_abst = [Res(f"abst{i}") for i in range(2)]
            qTs = sb("qTs", [128, 3, 32], BF16, esA)
            kTs = sb("kTs", [128, 3, 32], BF16, esA)
            R_qTs = [Res(f"qTs{g}") for g in range(3)]
            R_kTs = [Res(f"kTs{g}") for g in range(3)]
            kvnew = sb("kvnew", [8, 4, 2, 128], F32, esA)
            R_kvnew = Res("kvnew")
            Vnew = sb("Vnew", [8, 4, 128], BF16, esA)
            R_Vnew = Res("Vnew")
            ckb = [sb(f"ckb{i}", [128, 4, 128], BF16, esA) for i in range(2)]
            cvb = [sb(f"cvb{i}", [128, 4, 128], BF16, esA) for i in range(2)]
            R_ckb = [Res(f"ckb{i}") for i in range(2)]
            R_cvb = [Res(f"cvb{i}") for i in range(2)]
            sbs = sb("sbs", [128, 2, 4, 8], F32, esA)
            R_sbs = Res("sbs")
            PTs = sb("PTs", [128, 2, 4, 8], BF16, esA)
            R_PTs = Res("PTs")
            ACCs = sb("ACCs", [128, 2, 32], F32, esA)
            R_ACCs = Res("ACCs")
            fts = sb("fts", [128, 2, 32], F32, esA)
            R_fts = Res("fts")
            absst = sb("absst", [128, 32], BF16, esA)
            R_absst = Res("absst")

            items = [(hp, g) for hp in range(4) for g in range(3)]

            def load_item(ii, defer=False):
                hp, g = items[ii]
                sq, sk, sv = 24 + 4 * g + hp, 36 + 4 * g + hp, 48 + 4 * g + hp
                bi = ii % 2
                kb.dma("sp", bvk[bi][:, 0, :], brep_d[:, 4 * g + hp, 0, :], writes=[R_bvk[bi]])
                kb.dma("sp", bvk[bi][:, 1:5, :], brep_d[:, 12 + 4 * g + hp], writes=[R_bvk[bi]])
                if not defer:
                    return [slab_f32(WIN[s_]) for s_ in (sq, sk, sv)]
                (ra, da, ca), (rb_, db, cb_), (rc_, dc_, cc_) = [slab_f32(WIN[s_], True) for s_ in (sq, sk, sv)]
                da()
                db()
                deferred.append(lambda: None)
                deferred.append(lambda: (ca(), dc_()))
                deferred.append(lambda: None)
                deferred.append(cb_)
                deferred.append(cc_)
                return [ra, rb_, rc_]

            nxt = load_item(0)
            kb.dma("pool", xsTb[:], xsT, writes=[R_xsT])
            for t in range(8):
                kb.dma("pool", xTb[:, :, t * 512:(t + 1) * 512], xT[:, :, t * 512:(t + 1) * 512], writes=[R_xT[t]])
            scr_i = 0
            blk_it = 0
            for ii, (hp, g) in enumerate(items):
                win, d = GROUPS[g]
                nb = 32 // d
                (sq_t, Rsq), (sk_t, Rsk), (sv_t, Rsv) = nxt
                sq, sk = 24 + 4 * g + hp, 36 + 4 * g + hp
                bi = ii % 2
                if g == 0:
                    kb.dma("sp", bP[:], biasP_d[hp], writes=[R_bP])
                    kb.dma("sp", bS[:], biasS_d[hp], writes=[R_bS])
                    kb.dma("sp", bN[:], biasN_d[hp], writes=[R_bN])
                def unit_qk(which, tc):
                    if which == 0:
                        slab, Rs, dst, Rdst, bias_ap, Rb, scale = sq_t, Rsq, qT, R_qT, bq8[:, 4 * g + hp:4 * g + hp + 1], R_bq8, 0.125
                    else:
                        slab, Rs, dst, Rdst, bias_ap, Rb, scale = sk_t, Rsk, kT, R_kT, bcol[:, sk:sk + 1], R_bcol, 1.0
                    bk = pj_bank()
                    proj_fm(bk, 512, slab, Rs, lambda kc: xTb[:, kc, tc * 512:(tc + 1) * 512], [R_xT[tc]])
                    kb.op("act", lambda e: e.activation(out=dst[:, tc * 512:(tc + 1) * 512], in_=PS[bk][:, 0:512],
                                                        func=AF.Identity, bias=bias_ap, scale=scale),
                          reads=[R_PS[bk], Rb], writes=[Rdst[tc]])

                def unit_sqk(which):
                    if which == 0:
                        slab, Rs, dst, Rdst, bias_ap, Rb, scale = sq_t, Rsq, qTs, R_qTs, bq8[:, 4 * g + hp:4 * g + hp + 1], R_bq8, 0.125
                    else:
                        slab, Rs, dst, Rdst, bias_ap, Rb, scale = sk_t, Rsk, kTs, R_kTs, bcol[:, sk:sk + 1], R_bcol, 1.0
                    bk = pj_bank()
                    proj_fm(bk, 32, slab, Rs, lambda kc: xsTb[:, kc, :], [R_xsT])
                    kb.op("act", lambda e: e.activation(out=dst[:, g, :], in_=PS[bk][:, 0:32],
                                                        func=AF.Identity, bias=bias_ap, scale=scale),
                          reads=[R_PS[bk], Rb], writes=[Rdst[g]])

                def unit_v(q4):
                    bk = pj_bank()
                    for j in range(4):
                        blk = q4 * 4 + j
                        n, r = blk // d, blk % d
                        t0 = n * 128 * d + r
                        for kc in range(8):
                            kb.op("pe", lambda e: e.matmul(PS[bk][:, j * 128:(j + 1) * 128],
                                                            lhsT=xTb[:, kc, t0:t0 + 127 * d + 1:d], rhs=sv_t[:, kc, :],
                                                            start=(kc == 0), stop=(kc == 7)),
                                  reads=[Rsv] + [R_xT[c] for c in rngs(t0, t0 + 127 * d + 1)],
                                  writes=[R_PS[bk]], sig=(kc == 7))
                    kb.op("dve", lambda e: e.tensor_tensor(out=Vb[:, q4 * 4:q4 * 4 + 4, :],
                                                           in0=PS[bk][:, 0:512].rearrange("p (j c) -> p j c", j=4),
                                                           in1=bvk[bi][:, 1:5, :], op=ALU.add),
                          reads=[R_PS[bk], R_bvk[bi]], writes=[R_V[q4]])
                    for j in range(4):
                        blk = q4 * 4 + j
                        n, r = blk // d, blk % d
                        if n != nb - 1:
                            continue
                        t0 = n * 128 * d + r
                        ks = cnt.get("kvst", 0) % 2
                        cnt["kvst"] = cnt.get("kvst", 0) + 1
                        bk2 = pj_bank()
                        if bk2 == bk:
                            bk2 = pj_bank()
                        for kc in range(8):
                            kb.op("pe", lambda e: e.matmul(PS[bk2][:, 0:128],
                                                            lhsT=xTb[:, kc, t0:t0 + 127 * d + 1:d], rhs=sk_t[:, kc, :],
                                                            start=(kc == 0), stop=(kc == 7)),
                                  reads=[Rsk] + [R_xT[c] for c in rngs(t0, t0 + 127 * d + 1)],
                                  writes=[R_PS[bk2]], sig=(kc == 7))
                        kb.op("dve", lambda e: e.tensor_tensor(out=kvst[ks][:, 0, :], in0=PS[bk2][:, 0:128],
                                                               in1=bvk[bi][:, 0, :], op=ALU.add),
                              reads=[R_PS[bk2], R_bvk[bi]], writes=[R_kvst[ks]])
                        kb.op("dve", lambda e: e.tensor_tensor(out=kvst[ks][:, 1, :], in0=PS[bk][:, j * 128:(j + 1) * 128],
                                                               in1=bvk[bi][:, 1, :], op=ALU.add),
                              reads=[R_PS[bk], R_bvk[bi]], writes=[R_kvst[ks]])
                        kb.dma("sp", kvp[g][r:r + 127 * d + 1:d, :, hp * 128:(hp + 1) * 128], kvst[ks][:],
                               reads=[R_kvst[ks]], store=True)

                wb = win

                def unit_snew():
                    for b in range(4):
                        bk = pj_bank()
                        for which, slab, Rs in ((0, sk_t, Rsk), (1, sv_t, Rsv)):
                            for kc in range(8):
                                kb.op("pe", lambda e: e.matmul(PS[bk][0:8, which * 128:(which + 1) * 128],
                                                                lhsT=xsTb[:, kc, b * 8:(b + 1) * 8], rhs=slab[:, kc, :],
                                                                start=(kc == 0), stop=(kc == 7)),
                                      reads=[Rs, R_xsT], writes=[R_PS[bk]], sig=(kc == 7))
                        kb.op("dve", lambda e: e.tensor_tensor(out=kvnew[0:8, b], in0=PS[bk][0:8, 0:256].rearrange("p (k c) -> p k c", k=2),
                                                               in1=bvk[bi][0:8, 0:2, :], op=ALU.add),
                              reads=[R_PS[bk], R_bvk[bi]], writes=[R_kvnew])
                    kb.op("pool", lambda e: e.tensor_copy(Vnew[0:8], kvnew[0:8, :, 1, :]), reads=[R_kvnew], writes=[R_Vnew])
                    for b in range(4):
                        kb.dma("sp", kvs[g][b, wb - 8:wb, :, hp * 128:(hp + 1) * 128], kvnew[0:8, b],
                               reads=[R_kvnew], store=True)

                def emit_S(blk, par):
                    n, r = blk // d, blk % d
                    t0 = n * 128 * d + r
                    tq = slice(t0, t0 + 127 * d + 1, d)
                    pcs = (0, 1) if n > 0 else (1,)
                    for ab in range(2):
                        hs = slice(ab * 64, (ab + 1) * 64)
                        for pc in pcs:
                            tk0 = t0 - (1 - pc) * 128 * d
                            tk = slice(tk0, tk0 + 127 * d + 1, d)
                            kb.op("pe", lambda e: e.matmul(PS[ab * 2 + par][:, pc * 128:(pc + 1) * 128],
                                                            lhsT=kT[hs, tk], rhs=qT[hs, tq], start=True, stop=True),
                                  reads=[R_kT[c] for c in rngs(tk0, tk0 + 127 * d + 1)] +
                                        [R_qT[c] for c in rngs(t0, t0 + 127 * d + 1)],
                                  writes=[R_PS[ab * 2 + par]], sig=(pc == 1))

                def emit_B(blk, par):
                    n = blk // d
                    pcs = (0, 1) if n > 0 else (1,)
                    p0 = pcs[0]
                    for ab in range(2):
                        kb.op("dve", lambda e: e.tensor_tensor(
                            out=sbt[par][:, ab, p0:2, :],
                            in0=PS[ab * 2 + par][:, p0 * 128:256].rearrange("p (a c) -> p a c", c=128),
                            in1=bP[:, g, ab, p0:2, :], op=ALU.add),
                              reads=[R_PS[ab * 2 + par], R_bP], writes=[R_sbt[par]])
                    kb.op("act", lambda e: e.activation(out=PT[par][:, :, p0:2, :], in_=sbt[par][:, :, p0:2, :], func=AF.Exp),
                          reads=[R_sbt[par]], writes=[R_PT[par]])

                def emit_C(blk, par):
                    n, r = blk // d, blk % d
                    t0 = n * 128 * d + r
                    tq = slice(t0, t0 + 127 * d + 1, d)
                    pcs = (0, 1) if n > 0 else (1,)
                    p0 = pcs[0]
                    ob = 4 + par
                    for od in range(2):
                        for ab in range(2):
                            hs = slice(ab * 64, (ab + 1) * 64)
                            for pc in pcs:
                                vblk = blk - (1 - pc) * d
                                lhs = Vb[:, vblk, hs] if od == 0 else ones[:, 0:64]
                                rr = [R_PT[par], R_ones] + ([R_V[vblk // 4]] if od == 0 else [])
                                kb.op("pe", lambda e: e.matmul(PS[ob][hs, od * 128:(od + 1) * 128], lhsT=lhs,
                                                                rhs=PT[par][:, ab, pc, :], start=(pc == p0), stop=(pc == 1)),
                                      reads=rr, writes=[R_PS[ob]], sig=(od == 1 and ab == 1 and pc == 1))
                    cs = rngs(t0, t0 + 127 * d + 1)
                    src = PS[ob][:, 0:256].rearrange("p (a c) -> p a c", a=2)
                    if g == 0:
                        kb.op("act", lambda e: e.copy(ACC[:, :, tq], src),
                              reads=[R_PS[ob]], writes=[R_ACC[c] for c in cs])
                    else:
                        kb.op("dve", lambda e: e.tensor_tensor(out=ACC[:, :, tq], in0=src, in1=ACC[:, :, tq], op=ALU.add),
                              reads=[R_PS[ob]] + [R_ACC[c] for c in cs], writes=[R_ACC[c] for c in cs])

                def s_dma(rc):
                    cb = rc % 2
                    kb.dma("pool", ckb[cb][:], ckT[g][:, hp * 128:(hp + 1) * 128, rc * 128:(rc + 1) * 128].rearrange("b p r -> p b r"),
                           writes=[R_ckb[cb]])
                    kb.dma("pool", cvb[cb][:], cache[g][:, rc * 128:(rc + 1) * 128, 1, hp * 128:(hp + 1) * 128].rearrange("b p c -> p b c"),
                           writes=[R_cvb[cb]])

                def s_cast(rc):
                    pass

                def s_S(rc, par):
                    for ab in range(2):
                        hs = slice(ab * 64, (ab + 1) * 64)
                        for b in range(4):
                            if rc >= 0:
                                cb = rc % 2
                                kb.op("pe", lambda e: e.matmul(PS[ab * 2 + par][:, b * 8:(b + 1) * 8], lhsT=ckb[cb][hs, b, :],
                                                                rhs=qTs[hs, g, b * 8:(b + 1) * 8], start=True, stop=True),
                                      reads=[R_ckb[cb], R_qTs[g]], writes=[R_PS[ab * 2 + par]], sig=(b == 3))
                            else:
                                kb.op("pe", lambda e: e.matmul(PS[ab * 2 + par][0:8, b * 8:(b + 1) * 8], lhsT=kTs[hs, g, b * 8:(b + 1) * 8],
                                                                rhs=qTs[hs, g, b * 8:(b + 1) * 8], start=True, stop=True),
                                      reads=[R_kTs[g], R_qTs[g]], writes=[R_PS[ab * 2 + par]], sig=(b == 3))

                def s_B(rc, par):
                    for ab in range(2):
                        if rc >= 0:
                            kb.op("dve", lambda e: e.tensor_tensor(out=sbs[:, ab], in0=PS[ab * 2 + par][:, 0:32].rearrange("p (b t) -> p b t", b=4),
                                                                   in1=bS[:, CH0[g] + rc, ab], op=ALU.add),
                                  reads=[R_PS[ab * 2 + par], R_bS], writes=[R_sbs])
                        else:
                            kb.op("dve", lambda e: e.tensor_tensor(out=sbs[0:8, ab], in0=PS[ab * 2 + par][0:8, 0:32].rearrange("p (b t) -> p b t", b=4),
                                                                   in1=bN[0:8, g, ab], op=ALU.add),
                                  reads=[R_PS[ab * 2 + par], R_bN], writes=[R_sbs])
                    if rc >= 0:
                        kb.op("act", lambda e: e.activation(out=PTs[:], in_=sbs[:], func=AF.Exp), reads=[R_sbs], writes=[R_PTs])
                    else:
                        kb.op("act", lambda e: e.activation(out=PTs[0:8], in_=sbs[0:8], func=AF.Exp), reads=[R_sbs], writes=[R_PTs])

                def s_C(rc, par):
                    ob = 4 + par
                    for od in range(2):
                        for ab in range(2):
                            hs = slice(ab * 64, (ab + 1) * 64)
                            for b in range(4):
                                if rc >= 0:
                                    cb = rc % 2
                                    lhs = cvb[cb][:, b, hs] if od == 0 else ones[:, 0:64]
                                    kb.op("pe", lambda e: e.matmul(PS[ob][hs, od * 32 + b * 8: od * 32 + (b + 1) * 8], lhsT=lhs,
                                                                    rhs=PTs[:, ab, b, :], start=True, stop=True),
                                          reads=[R_PTs, R_cvb[cb], R_ones], writes=[R_PS[ob]],
                                          sig=(od == 1 and ab == 1 and b == 3))
                                else:
                                    lhs = Vnew[0:8, b, hs] if od == 0 else ones[0:8, 0:64]
                                    kb.op("pe", lambda e: e.matmul(PS[ob][hs, od * 32 + b * 8: od * 32 + (b + 1) * 8], lhsT=lhs,
                                                                    rhs=PTs[0:8, ab, b, :], start=True, stop=True),
                                          reads=[R_PTs, R_Vnew, R_ones], writes=[R_PS[ob]],
                                          sig=(od == 1 and ab == 1 and b == 3))
                    src = PS[ob][:, 0:64].rearrange("p (a c) -> p a c", a=2)
                    if g == 0 and rc == 0:
                        kb.op("dve", lambda e: e.tensor_copy(ACCs[:], src), reads=[R_PS[ob]], writes=[R_ACCs])
                    else:
                        kb.op("dve", lambda e: e.tensor_tensor(out=ACCs[:], in0=src, in1=ACCs[:], op=ALU.add),
                              reads=[R_PS[ob], R_ACCs], writes=[R_ACCs])

                def st_S(ent):
                    (emit_S if ent[0] == "p" else s_S)(ent[1], ent[2])

                def st_B(ent):
                    (emit_B if ent[0] == "p" else s_B)(ent[1], ent[2])

                def st_C(ent):
                    (emit_C if ent[0] == "p" else s_C)(ent[1], ent[2])

                pipe = []

                def push(kind, idx):
                    nonlocal blk_it, scr_i, dd_i
                    par = blk_it % 2
                    blk_it += 1
                    ent = (kind, idx, par)
                    st_S(ent)
                    if len(pipe) >= 1:
                        st_B(pipe[-1])
                    if len(pipe) >= 2:
                        st_C(pipe[-2])
                        pipe.pop(0)
                    pipe.append(ent)
                    if kind != "p":
                        return
                    blk = idx
                    if blk % 6 == 5 and scr_i < len(scr_jobs):
                        jdx, src, nk = scr_jobs[scr_i]
                        scr_i += 1
                        to_scratch(jdx, src[:, 0:nk, :], nk)
                    if blk % 8 == 3 and dd_i < len(dd_jobs):
                        kb.dma("sp", dd_jobs[dd_i][0], dd_jobs[dd_i][1], reads=[R_dd], store=True)
                        dd_i += 1

                def flush():
                    if len(pipe) == 2:
                        st_B(pipe[1])
                        st_C(pipe[0])
                        st_C(pipe[1])
                    elif len(pipe) == 1:
                        st_B(pipe[0])
                        st_C(pipe[0])
                    pipe.clear()

                units = [(-1, lambda: unit_sqk(0)), (-1, lambda: unit_sqk(1)), (-1, unit_snew)]
                for tc in range(8):
                    units.append((tc, lambda tc=tc: unit_qk(0, tc)))
                    units.append((tc, lambda tc=tc: unit_qk(1, tc)))
                    units.append((tc, lambda tc=tc: unit_v(tc)))
                if g < 2:
                    ready_after = {tc: list(range(4 * tc, 4 * tc + 4)) for tc in range(8)}
                else:
                    ready_after = {3: list(range(0, 16)), 7: list(range(16, 32))}
                ready = []
                ui = 0
                prefetched = False
                s_list = list(range(NCH[g])) + [-1]
                s_every = max(1, 32 // len(s_list))
                s_next = 0
                s_dma(0)

                def emit_unit():
                    nonlocal ui, nxt, prefetched
                    tcu, fn = units[ui]
                    fn()
                    ui += 1
                    if ui == len(units) or units[ui][0] != tcu:
                        if tcu in ready_after:
                            ready.extend(ready_after[tcu])
                    if ui == len(units) and not prefetched:
                        prefetched = True
                        nxt = load_item(ii + 1, True) if ii + 1 < len(items) else None

                def push_sample():
                    nonlocal s_next
                    rc = s_list[s_next]
                    s_next += 1
                    if rc >= 0:
                        s_cast(rc)
                    push("s", rc)
                    if s_next < len(s_list) and s_list[s_next] >= 0:
                        s_dma(s_list[s_next])

                nblk = 0
                while ui < len(units) or ready:
                    if not ready:
                        emit_unit()
                        continue
                    push("p", ready.pop(0))
                    run_deferred()
                    nblk += 1
                    if nblk % s_every == 0 and s_next < len(s_list) and ui >= 3:
                        push_sample()
                    if ui < len(units):
                        emit_unit()
                while s_next < len(s_list):
                    push_sample()
                    run_deferred()
                flush()
                run_deferred(len(deferred))

                if g == 2:
                    zs = 60 + hp
                    zb_t, Rzb = slab_f32(WIN[zs])
                    kb.op("act", lambda e: e.activation(out=ACC[:, 1, :], in_=ACC[:, 1, :], func=AF.Ln), reads=R_ACC, writes=R_ACC)
                    kb.op("act", lambda e: e.activation(out=ACC[:, 1, :], in_=ACC[:, 1, :], func=AF.Exp, scale=-1.0),
                          reads=R_ACC, writes=R_ACC)
                    for tc in range(8):
                        fb = tc % 2
                        bk = pj_bank()
                        proj_fm(bk, 512, zb_t, Rzb, lambda kc: xTb[:, kc, tc * 512:(tc + 1) * 512], [R_xT[tc]])
                        kb.op("act", lambda e: e.activation(out=ftmp[2][:], in_=PS[bk][:, 0:512], func=AF.Silu,
                                                            bias=bcol[:, zs:zs + 1], scale=1.0),
                              reads=[R_PS[bk], R_bcol], writes=[R_ftmp[2]])
                        sl = slice(tc * 512, (tc + 1) * 512)
                        kb.op("dve", lambda e: e.tensor_tensor(out=ftmp[fb][:], in0=ACC[:, 0, sl], in1=ACC[:, 1, sl], op=ALU.mult),
                              reads=[R_ACC[tc]], writes=[R_ftmp[fb]])
                        kb.op("dve", lambda e: e.tensor_tensor(out=abst[fb][:], in0=ftmp[fb][:], in1=ftmp[2][:], op=ALU.mult),
                              reads=[R_ftmp[fb], R_ftmp[2]], writes=[R_abst[fb]])
                        kb.dma("sp", ABS[:, hp, sl], abst[fb][:], reads=[R_abst[fb]], writes=[R_abs[hp][tc]], store=True)
                    bk = pj_bank()
                    proj_fm(bk, 32, zb_t, Rzb, lambda kc: xsTb[:, kc, :], [R_xsT])
                    kb.op("act", lambda e: e.activation(out=fts[:, 1, :], in_=PS[bk][:, 0:32], func=AF.Silu,
                                                        bias=bcol[:, zs:zs + 1], scale=1.0),
                          reads=[R_PS[bk], R_bcol], writes=[R_fts])
                    kb.op("dve", lambda e: e.reciprocal(fts[:, 0, :], ACCs[:, 1, :]), reads=[R_ACCs], writes=[R_fts])
                    kb.op("dve", lambda e: e.tensor_tensor(out=fts[:, 0, :], in0=ACCs[:, 0, :], in1=fts[:, 0, :], op=ALU.mult),
                          reads=[R_ACCs, R_fts], writes=[R_fts])
                    kb.op("dve", lambda e: e.tensor_tensor(out=absst[:], in0=fts[:, 0, :], in1=fts[:, 1, :], op=ALU.mult),
                          reads=[R_fts], writes=[R_absst])
                    kb.dma("sp", ABSs[:, hp, :], absst[:], reads=[R_absst], writes=[R_abss[hp]], store=True)

            while dd_i < len(dd_jobs):
                kb.dma("sp", dd_jobs[dd_i][0], dd_jobs[dd_i][1], reads=[R_dd], store=True)
                dd_i += 1
            while scr_i < len(scr_jobs):
                idx, src, nk = scr_jobs[scr_i]
                scr_i += 1
                to_scratch(idx, src[:, 0:nk, :], nk)
            kb.barrier()

        esB = ExitStack()
        with esB:
            Ub = sb("Ub", [128, 8, 542], BF16, esB)
            R_U = [Res(f"U{c}") for c in range(8)]
            Utail = sb("Utail", [128, 8, 30], F32, esB)
            R_Utail = Res("Utail")
            Us = sb("Us", [128, 8, 4, 38], F32, esB)
            Usb = sb("Usb", [128, 8, 4, 38], BF16, esB)
            R_Us = [Res(f"Us{c}") for c in range(8)]
            R_Usb = [Res(f"Usb{c}") for c in range(8)]
            Dg = [sb(f"Dg{i}", [128, 31, 128], BF16, esB) for i in range(2)]
            R_Dg = [[Res(f"Dg{i}a"), Res(f"Dg{i}b")] for i in range(2)]
            KS = 16
            identb = sb("identb", [128, 128], BF16, esB)
            R_ident = Res("ident")
            acc = sb("acc", [128, 8, 512], F32, esB)
            R_acc = [Res(f"acc{c}") for c in range(8)]
            sig_t = [sb(f"sig{i}", [128, 512], F32, esB) for i in range(2)]
            R_sig = [Res(f"sig{i}") for i in range(2)]
            vb_t = [sb(f"vb{i}", [128, 2, 512], BF16, esB) for i in range(2)]
            R_vb = [Res(f"vb{i}") for i in range(2)]
            lnt = sb("lnt", [128, 2, 512], F32, esB)
            R_lnt = Res("lnt")
            sz_t = [sb(f"sz{i}", [128, 512], F32, esB) for i in range(2)]
            R_sz = [Res(f"sz{i}") for i in range(2)]
            caT = sb("caT", [128, 8, 512], BF16, esB)
            R_caT = [Res(f"caT{c}") for c in range(8)]
            mT = sb("mT", [128, 8, 512], BF16, esB)
            R_mT = [Res(f"mT{c}") for c in range(8)]
            abt = [sb(f"abt{i}", [128, 4, 512], BF16, esB) for i in range(1)]
            R_abt = [Res(f"abt{i}") for i in range(1)]
            woutb = sb("woutb", [128, 8, 1024], BF16, esB)
            R_wout = Res("wout")
            lngb = sb("lngb", [128, 2, 1024], F32, esB)
            R_lngb = Res("lngb")
            convw = sb("convw", [128, 8, 31], F32, esB)
            cvec = sb("cvec", [128, 3, 8], F32, esB)
            R_cw = Res("cw")
            xt = [sb(f"xt{i}", [128, 1024], F32, esB) for i in range(1)]
            R_xt = [Res(f"xt{i}") for i in range(1)]
            rb = [sb(f"rb{i}", [128, 1024], F32, esB) for i in range(2)]
            R_rb = [Res(f"rb{i}") for i in range(2)]
            negh = sb("negh", [128, 1], F32, esB)
            R_negh = Res("negh")
            kb.op("pool", lambda e: e.memset(negh[:], -0.5), writes=[R_negh])
            st = sb("st", [128, 4, 8], F32, esB)
            R_st = [Res(f"st{i}") for i in range(4)]

            kb.dma("sp", convw[:], convw_d, writes=[R_cw])
            kb.dma("sp", cvec[:], cvec_d, writes=[R_cw])
            kb.dma("sp", lngb[:], lngb_d, writes=[R_lngb])
            for c in range(8):
                kb.dma("sp", Us[:, c, :, 0:30], scT[:, c], writes=[R_Us[c]])
            kb.op("pool", lambda e: e.memset(Ub[:, :, 0:30], 0.0), writes=R_U)
            kb.dma("pool", identb[:], ident_d, writes=[R_ident])
            for dc in range(8):
                kb.dma("pool", woutb[:, dc, :], WO[:, dc, :], writes=[R_wout])
            wo_steps = []
            pend = []

            def s4_x(M, j, xsrc, ydst):
                kb.dma("sp", xt[0][0:M, :], xsrc, writes=[R_xt[0]])
                rs = cnt.get("rs", 0) % 2
                cnt["rs"] = cnt.get("rs", 0) + 1
                return rs

            def s4_o(M, j, xsrc, ydst, rs, half):
                bo = 5
                for dc in range(8):
                    kb.op("pe", lambda e: e.matmul(PS[bo][0:M, 0:512], lhsT=mT[:, dc, j * 128: j * 128 + M],
                                                    rhs=woutb[:, dc, half * 512:(half + 1) * 512],
                                                    start=(dc == 0), stop=(dc == 7)),
                          reads=[R_mT[dc], R_wout], writes=[R_PS[bo]], sig=(dc == 7))

            def s4_r(M, j, xsrc, ydst, rs, half):
                bo = 5
                rv = rb[rs][0:M, :]
                kb.op("dve", lambda e: e.scalar_tensor_tensor(out=rv[:, half * 512:(half + 1) * 512],
                                                              in0=xt[0][0:M, half * 512:(half + 1) * 512], scalar=ALPHA,
                                                              in1=PS[bo][0:M, 0:512], op0=ALU.mult, op1=ALU.add),
                      reads=[R_xt[0], R_PS[bo]], writes=[R_rb[rs]])

            def s4_stats(M, j, xsrc, ydst, rs):
                rv = rb[rs][0:M, :]
                Rr = [R_rb[rs]]
                Rst = R_st[rs]
                kb.op("act", lambda e: e.activation(out=xt[0][0:M, :], in_=rv, func=AF.Square, accum_out=st[0:M, rs, 1:2]),
                      reads=Rr, writes=[R_xt[0], Rst])
                kb.op("act", lambda e: e.activation(out=rv, in_=rv, func=AF.Identity, accum_out=st[0:M, rs, 0:1]),
                      reads=Rr, writes=Rr + [Rst])

            def s4_fin(M, j, xsrc, ydst, rs):
                rv = rb[rs][0:M, :]
                Rr = [R_rb[rs]]
                Rst = R_st[rs]
                s_mean, s_tmp, s_rstd, s_nmr = (st[0:M, rs, i:i + 1] for i in range(2, 6))
                kb.op("dve", lambda e: e.tensor_scalar(s_mean, st[0:M, rs, 0:1], 1.0 / 1024, None, op0=ALU.mult), reads=[Rst], writes=[Rst])
                kb.op("dve", lambda e: e.tensor_tensor(out=s_tmp, in0=s_mean, in1=s_mean, op=ALU.mult), reads=[Rst], writes=[Rst])
                kb.op("dve", lambda e: e.scalar_tensor_tensor(out=s_tmp, in0=st[0:M, rs, 1:2], scalar=1.0 / 1024, in1=s_tmp,
                                                              op0=ALU.mult, op1=ALU.subtract), reads=[Rst], writes=[Rst])
                kb.op("dve", lambda e: e.tensor_scalar(s_tmp, s_tmp, LN_EPS, None, op0=ALU.add), reads=[Rst], writes=[Rst])
                kb.op("pool", lambda e: e.tensor_tensor(out=s_rstd, in0=s_tmp, in1=negh[0:M, :], op=ALU.pow),
                      reads=[Rst, R_negh], writes=[Rst])
                kb.op("dve", lambda e: e.scalar_tensor_tensor(out=s_nmr, in0=s_mean, scalar=-1.0, in1=s_rstd,
                                                              op0=ALU.mult, op1=ALU.mult), reads=[Rst], writes=[Rst])
                kb.op("dve", lambda e: e.tensor_scalar(rv, rv, s_rstd, s_nmr, op0=ALU.mult, op1=ALU.add),
                      reads=Rr + [Rst], writes=Rr)
                kb.op("pool", lambda e: e.tensor_tensor(out=rv, in0=rv, in1=lngb[0:M, 0, :], op=ALU.mult),
                      reads=Rr + [R_lngb], writes=Rr)
                kb.op("pool", lambda e: e.tensor_tensor(out=rv, in0=rv, in1=lngb[0:M, 1, :], op=ALU.add),
                      reads=Rr + [R_lngb], writes=Rr)
                kb.dma("sp", ydst, rv, reads=Rr, store=True)

            def stage4_sub(M, j, xsrc, ydst):
                a = (M, j, xsrc, ydst)
                rs = s4_x(*a)
                for half in range(2):
                    s4_o(*a, rs, half)
                    s4_r(*a, rs, half)
                s4_stats(*a, rs)
                s4_fin(*a, rs)

            pre = {}

            def prefetch(idx, nk=8):
                if idx not in pre:
                    pre[idx] = slab_scr(idx, nk)

            def get_slab(idx, nk=8):
                if idx in pre:
                    return pre.pop(idx)
                return slab_scr(idx, nk)

            tiles = [("p", ti) for ti in range(8)] + [("s", 0)]
            dcnt = [0]
            vg_pre = {}
            z_pre = []
            for kind, ti in tiles:
                if kind == "p":
                    N = 512
                    t0 = ti * 512
                    xr = lambda kc: xTb[:, kc, t0:t0 + 512]
                    xres = [R_xT[ti]]
                    RU = R_U
                    uv = lambda cc, k: Ub[:, cc, k:k + 512]
                    v3 = lambda ap: ap
                else:
                    N = 32
                    xr = lambda kc: xsTb[:, kc, :]
                    xres = [R_xsT]
                    RU = R_Usb
                    uv = lambda cc, k: Usb[:, cc, :, k:k + 8]
                    v3 = lambda ap: ap.rearrange("p (s l) -> p s l", s=4)

                ab_i = 0
                if kind == "p":
                    kb.dma("sp", abt[ab_i][:], ABS[:, :, t0:t0 + 512], reads=[R_abs[h][ti] for h in range(4)], writes=[R_abt[ab_i]])
                else:
                    kb.dma("sp", abt[ab_i][:, :, 0:32], ABSs, reads=R_abss, writes=[R_abt[ab_i]])

                S1, S2 = 6, 7
                st1 = {}

                def VG(cc):
                    st1[cc] = (slab_scr(cc), slab_scr(8 + cc))

                def VGmm(cc):
                    (sv_t, Rv), (sg_t, Rg) = st1[cc]
                    proj_fm(cc % 2, N, sv_t, Rv, xr, xres)
                    proj_fm(2 + cc % 2, N, sg_t, Rg, xr, xres)

                def gen_D(cc):
                    di = dcnt[0] % 2
                    dcnt[0] += 1
                    st1[("d", cc)] = di
                    kb.op("dve", lambda e: e.tensor_tensor(out=Dg[di][:, 0:KS, :],
                                                           in0=identb[:].unsqueeze(1).to_broadcast([128, KS, 128]),
                                                           in1=convw[:, cc, 0:KS].unsqueeze(2).to_broadcast([128, KS, 128]), op=ALU.mult),
                          reads=[R_ident, R_cw], writes=[R_Dg[di][0]])
                    kb.op("pool", lambda e: e.tensor_tensor(out=Dg[di][:, KS:31, :],
                                                            in0=identb[:].unsqueeze(1).to_broadcast([128, 31 - KS, 128]),
                                                            in1=convw[:, cc, KS:31].unsqueeze(2).to_broadcast([128, 31 - KS, 128]), op=ALU.mult),
                          reads=[R_ident, R_cw], writes=[R_Dg[di][1]])

                def stats(cc):
                    si = cc % 2
                    kb.op("pe", lambda e: e.matmul(PS[S1][:, 0:N], lhsT=ones[:], rhs=vb_t[si][:, 0, 0:N],
                                                    start=(cc == 0), stop=(cc == 7)),
                          reads=[R_ones, R_vb[si]], writes=[R_PS[S1]], sig=False)
                    kb.op("pe", lambda e: e.matmul(PS[S2][:, 0:N], lhsT=ones[:], rhs=vb_t[si][:, 1, 0:N],
                                                    start=(cc == 0), stop=(cc == 7)),
                          reads=[R_ones, R_vb[si]], writes=[R_PS[S2]], sig=True)

                if kind == "p":
                    if vg_pre:
                        st1.update(vg_pre)
                        vg_pre.clear()
                    else:
                        VG(0)
                        VG(1)
                    VGmm(0)
                    gen_D(0)
                    cur4 = None
                    fin4 = None
                    for cc in range(8):
                        if cc + 2 < 8:
                            VG(cc + 2)
                        if cc + 1 < 8:
                            VGmm(cc + 1)
                        if cc % 2 == 0 and pend:
                            cur4 = pend.pop(0)
                            cur4 = cur4 + (s4_x(*cur4),)
                            s4_o(*cur4, 0)
                        elif cc % 2 == 1 and cur4 is not None:
                            s4_o(*cur4, 1)
                        bv, bg, cbk = cc % 2, 2 + cc % 2, 4
                        si = cc % 2
                        kb.op("act", lambda e: e.activation(out=sig_t[si][:, 0:N], in_=PS[bg][:, 0:N], func=AF.Sigmoid,
                                                            bias=bcol[:, 8 + cc:9 + cc], scale=1.0),
                              reads=[R_PS[bg], R_bcol], writes=[R_sig[si]])
                        if kind == "p":
                            kb.op("dve", lambda e: e.scalar_tensor_tensor(out=Ub[:, cc, 30:542], in0=PS[bv][:, 0:N], scalar=bcol[:, cc:cc + 1],
                                                                          in1=sig_t[si][:, 0:N], op0=ALU.add, op1=ALU.mult),
                                  reads=[R_PS[bv], R_bcol, R_sig[si]], writes=[R_U[cc]])
                            if ti == 7:
                                kb.op("dve", lambda e: e.scalar_tensor_tensor(out=Utail[:, cc, :], in0=PS[bv][:, 482:512], scalar=bcol[:, cc:cc + 1],
                                                                              in1=sig_t[si][:, 482:512], op0=ALU.add, op1=ALU.mult),
                                      reads=[R_PS[bv], R_bcol, R_sig[si]], writes=[R_Utail])
                        else:
                            kb.op("dve", lambda e: e.scalar_tensor_tensor(out=Us[:, cc, :, 30:38], in0=v3(PS[bv][:, 0:N]), scalar=bcol[:, cc:cc + 1],
                                                                          in1=v3(sig_t[si][:, 0:N]), op0=ALU.add, op1=ALU.mult),
                                  reads=[R_PS[bv], R_bcol, R_sig[si]], writes=[R_Us[cc]])
                            kb.op("dve", lambda e: e.tensor_copy(Usb[:, cc], Us[:, cc]), reads=[R_Us[cc]], writes=[R_Usb[cc]])
                        if cc + 1 < 8:
                            gen_D(cc + 1)
                        di = st1[("d", cc)]
                        for k in range(31):
                            kb.op("pe", lambda e: e.matmul(v3(PS[cbk][:, 0:N]), lhsT=Dg[di][:, k, :], rhs=uv(cc, k),
                                                            start=(k == 0), stop=(k == 30)),
                                  reads=[R_Dg[di][0 if k < KS else 1], RU[cc]], writes=[R_PS[cbk]], sig=(k == 30))
                        cbias = cvec[:, 0, cc:cc + 1]
                        kb.op("act", lambda e: e.activation(out=acc[:, cc, 0:N], in_=PS[cbk][:, 0:N], func=AF.Identity, bias=cbias, scale=1.0),
                              reads=[R_PS[cbk], R_cw], writes=[R_acc[cc]])
                        kb.op("act", lambda e: e.activation(out=vb_t[si][:, 0, 0:N], in_=PS[cbk][:, 0:N], func=AF.Identity, bias=cbias, scale=1.0),
                              reads=[R_PS[cbk], R_cw], writes=[R_vb[si]])
                        kb.op("act", lambda e: e.activation(out=vb_t[si][:, 1, 0:N], in_=PS[cbk][:, 0:N], func=AF.Square, bias=cbias, scale=1.0),
                              reads=[R_PS[cbk], R_cw], writes=[R_vb[si]])
                        if cc >= 1:
                            stats(cc - 1)
                        if fin4 is not None:
                            s4_fin(*fin4)
                            fin4 = None
                        if cur4 is not None:
                            if cc % 2 == 0:
                                s4_r(*cur4, 0)
                            else:
                                s4_r(*cur4, 1)
                                s4_stats(*cur4)
                                fin4 = cur4
                                cur4 = None
                        if cc == 5:
                            prefetch(16)
                        elif cc == 6:
                            prefetch(24)
                            prefetch(32)
                        elif cc == 7:
                            prefetch(17)
                            prefetch(25)
                            prefetch(33)
                    stats(7)
                    if fin4 is not None:
                        s4_fin(*fin4)
                        fin4 = None
                    while wo_steps:
                        for f_ in wo_steps.pop(0):
                            f_()
                    while pend:
                        stage4_sub(*pend.pop(0))
                else:
                    subs4 = [pend.pop(0) for _ in range(len(pend))]
                    st4 = []
                    for a_ in subs4:
                        st4.append([a_, None])
                    steps4 = []
                    nsub = len(st4)
                    for p_ in range(2 * nsub + 2):
                        stp = []
                        if p_ >= 2 and (p_ - 2) % 2 == 0 and (p_ - 2) // 2 < nsub:
                            stp.append(("C", (p_ - 2) // 2))
                        if p_ >= 3 and (p_ - 3) % 2 == 0 and (p_ - 3) // 2 < nsub:
                            stp.append(("D", (p_ - 3) // 2))
                        if p_ >= 1 and (p_ - 1) % 2 == 0 and (p_ - 1) // 2 < nsub:
                            stp.append(("B", (p_ - 1) // 2))
                        if p_ % 2 == 0 and p_ // 2 < nsub:
                            stp.append(("A", p_ // 2))
                        steps4.append(stp)

                    def run_pend():
                        if not steps4:
                            return
                        for kind4, j4 in steps4.pop(0):
                            a_, rs_ = st4[j4]
                            if kind4 == "A":
                                st4[j4][1] = s4_x(*a_)
                                s4_o(*a_, st4[j4][1], 0)
                            elif kind4 == "B":
                                s4_r(*a_, rs_, 0)
                                s4_o(*a_, rs_, 1)
                            elif kind4 == "C":
                                s4_r(*a_, rs_, 1)
                                s4_stats(*a_, rs_)
                            else:
                                s4_fin(*a_, rs_)
                    for half in range(2):
                        ccs = list(range(4 * half, 4 * half + 4))
                        sl = {cc: (slab_scr(cc), slab_scr(8 + cc)) for cc in ccs}
                        for cc in ccs:
                            co = (cc % 4) * 32
                            for (slab_, Rs_), bank in ((sl[cc][0], half), (sl[cc][1], 2 + half)):
                                for kc in range(8):
                                    kb.op("pe", lambda e: e.matmul(PS[bank][:, co:co + 32], lhsT=slab_[:, kc, :], rhs=xsTb[:, kc, :],
                                                                    start=(kc == 0), stop=(kc == 7)),
                                          reads=[Rs_, R_xsT], writes=[R_PS[bank]], sig=(kc == 7))
                        for cc in ccs:
                            co = (cc % 4) * 32
                            kb.op("act", lambda e: e.activation(out=sig_t[0][:, cc * 32:(cc + 1) * 32], in_=PS[2 + half][:, co:co + 32],
                                                                func=AF.Sigmoid, bias=bcol[:, 8 + cc:9 + cc], scale=1.0),
                                  reads=[R_PS[2 + half], R_bcol], writes=[R_sig[0]])
                        for cc in ccs:
                            co = (cc % 4) * 32
                            kb.op("dve", lambda e: e.scalar_tensor_tensor(out=Us[:, cc, :, 30:38], in0=v3(PS[half][:, co:co + 32]),
                                                                          scalar=bcol[:, cc:cc + 1], in1=v3(sig_t[0][:, cc * 32:(cc + 1) * 32]),
                                                                          op0=ALU.add, op1=ALU.mult),
                                  reads=[R_PS[half], R_bcol, R_sig[0]], writes=[R_Us[cc]])
                        run_pend()
                    accv = acc[:, :, 0:32].rearrange("p c (s l) -> p c s l", s=4)
                    prodv = sig_t[1][:, 0:256].rearrange("p (c s l) -> p c s l", c=8, s=4)
                    wbc = lambda k: convw[:, :, k:k + 1].unsqueeze(3).to_broadcast([128, 8, 4, 8])
                    kb.op("dve", lambda e: e.tensor_tensor(out=accv, in0=Us[:, :, :, 0:8], in1=wbc(0), op=ALU.mult),
                          reads=R_Us + [R_cw], writes=R_acc)
                    for k in range(1, 31):
                        kb.op("dve", lambda e: e.tensor_tensor(out=prodv, in0=Us[:, :, :, k:k + 8], in1=wbc(k), op=ALU.mult),
                              reads=R_Us + [R_cw], writes=[R_sig[1]])
                        kb.op("dve", lambda e: e.tensor_tensor(out=accv, in0=accv, in1=prodv, op=ALU.add),
                              reads=R_acc + [R_sig[1]], writes=R_acc)
                        if k % 4 == 0:
                            run_pend()
                    kb.op("dve", lambda e: e.tensor_tensor(out=accv, in0=accv,
                                                           in1=cvec[:, 0, :].unsqueeze(2).unsqueeze(3).to_broadcast([128, 8, 4, 8]), op=ALU.add),
                          reads=R_acc + [R_cw], writes=R_acc)
                    vview = lambda i: vb_t[0][:, i, 0:256].rearrange("p (c n) -> p c n", c=8)
                    kb.op("act", lambda e: e.activation(out=vview(0), in_=acc[:, :, 0:32], func=AF.Copy), reads=R_acc, writes=[R_vb[0]])
                    kb.op("act", lambda e: e.activation(out=vview(1), in_=acc[:, :, 0:32], func=AF.Square), reads=R_acc, writes=[R_vb[0]])
                    for cc in range(8):
                        kb.op("pe", lambda e: e.matmul(PS[S1][:, 0:32], lhsT=ones[:], rhs=vb_t[0][:, 0, cc * 32:(cc + 1) * 32],
                                                        start=(cc == 0), stop=(cc == 7)),
                              reads=[R_ones, R_vb[0]], writes=[R_PS[S1]], sig=(cc == 7))
                    for cc in range(8):
                        kb.op("pe", lambda e: e.matmul(PS[S2][:, 0:32], lhsT=ones[:], rhs=vb_t[0][:, 1, cc * 32:(cc + 1) * 32],
                                                        start=(cc == 0), stop=(cc == 7)),
                              reads=[R_ones, R_vb[0]], writes=[R_PS[S2]], sig=(cc == 7))
                    while steps4:
                        run_pend()
                if kind == "p":
                    if ti == 7:
                        for c in range(0, 8, 2):
                            kb.dma("sp", convp[:, c:c + 2], Utail[:, c:c + 2, :], reads=[R_Utail], store=True)
                    else:
                        kb.op("pool", lambda e: e.tensor_copy(Ub[:, :, 0:30], Ub[:, :, 512:542]), reads=R_U, writes=R_U)
                else:
                    for c in range(8):
                        kb.dma("sp", convs[:, c], Us[:, c, :, 8:38], reads=[R_Us[c]], store=True)
                mean, tmp = sig_t[0][:, 0:N], sig_t[1][:, 0:N]
                Rmt = [R_sig[0], R_sig[1]]
                rstd, nmr = lnt[:, 0, 0:N], lnt[:, 1, 0:N]
                kb.op("act", lambda e: e.mul(mean, PS[S1][:, 0:N], 1.0 / 1024), reads=[R_PS[S1]], writes=[R_sig[0]])
                kb.op("dve", lambda e: e.tensor_tensor(out=tmp, in0=mean, in1=mean, op=ALU.mult), reads=[R_sig[0]], writes=[R_sig[1]])
                kb.op("dve", lambda e: e.scalar_tensor_tensor(out=tmp, in0=PS[S2][:, 0:N], scalar=1.0 / 1024, in1=tmp,
                                                              op0=ALU.mult, op1=ALU.subtract),
                      reads=[R_PS[S2], R_sig[1]], writes=[R_sig[1]])
                kb.op("dve", lambda e: e.tensor_scalar(tmp, tmp, LN_EPS, None, op0=ALU.add), reads=[R_sig[1]], writes=[R_sig[1]])
                kb.op("act", lambda e: e.sqrt(tmp, tmp), reads=[R_sig[1]], writes=[R_sig[1]])
                kb.op("dve", lambda e: e.reciprocal(rstd, tmp), reads=[R_sig[1]], writes=[R_lnt])
                kb.op("dve", lambda e: e.scalar_tensor_tensor(out=nmr, in0=mean, scalar=-1.0, in1=rstd, op0=ALU.mult, op1=ALU.mult),
                      reads=[R_sig[0], R_lnt], writes=[R_lnt])

                def gg_proj(dc):
                    ga_t, Rga = get_slab(24 + dc)
                    gb_t, Rgb = get_slab(32 + dc)
                    proj_fm(4 + dc % 2, N, ga_t, Rga, xr, xres)
                    proj_fm(6 + dc % 2, N, gb_t, Rgb, xr, xres)

                for cc in range(8):
                    if cc == 2:
                        gg_proj(0)
                        gg_proj(1)
                    sz_s, Rz = get_slab(16 + cc)
                    if cc + 1 < 8:
                        prefetch(16 + cc + 1)
                    if cc == 4:
                        prefetch(40)
                        prefetch(48, 4)
                    bz = cc % 2
                    si = cc % 2
                    proj_fm(bz, N, sz_s, Rz, xr, xres)
                    kb.op("act", lambda e: e.activation(out=sz_t[si][:, 0:N], in_=PS[bz][:, 0:N], func=AF.Silu,
                                                        bias=bcol[:, 16 + cc:17 + cc], scale=1.0),
                          reads=[R_PS[bz], R_bcol], writes=[R_sz[si]])
                    a = acc[:, cc, 0:N]
                    gsc = cvec[:, 1, cc:cc + 1]
                    kb.op("dve", lambda e: e.scalar_tensor_tensor(out=a, in0=a, scalar=gsc, in1=rstd, op0=ALU.mult, op1=ALU.mult),
                          reads=[R_acc[cc], R_lnt, R_cw], writes=[R_acc[cc]])
                    kb.op("dve", lambda e: e.scalar_tensor_tensor(out=a, in0=nmr, scalar=gsc, in1=a, op0=ALU.mult, op1=ALU.add),
                          reads=[R_acc[cc], R_lnt, R_cw], writes=[R_acc[cc]])
                    kb.op("act", lambda e: e.activation(out=a, in_=a, func=AF.Silu, bias=cvec[:, 2, cc:cc + 1], scale=1.0),
                          reads=[R_acc[cc], R_cw], writes=[R_acc[cc]])
                    kb.op("pool", lambda e: e.tensor_tensor(out=caT[:, cc, 0:N], in0=a, in1=sz_t[si][:, 0:N], op=ALU.mult),
                          reads=[R_acc[cc], R_sz[si]], writes=[R_caT[cc]])

                for dc in range(8):
                    wa_t, Rwa = get_slab(40 + dc)
                    wb_t, Rwb = get_slab(48 + dc, 4)
                    if dc + 1 < 8:
                        prefetch(40 + dc + 1)
                        prefetch(48 + dc + 1, 4)
                    bpa, bpb, bga, bgb = dc % 2, 2 + dc % 2, 4 + dc % 2, 6 + dc % 2
                    for cc in range(8):
                        kb.op("pe", lambda e: e.matmul(PS[bpa][:, 0:N], lhsT=wa_t[:, cc, :], rhs=caT[:, cc, 0:N],
                                                        start=(cc == 0), stop=(cc == 7)),
                              reads=[Rwa, R_caT[cc]], writes=[R_PS[bpa]], sig=(cc == 7))
                    for fc in range(4):
                        kb.op("pe", lambda e: e.matmul(PS[bpb][:, 0:N], lhsT=wb_t[:, fc, :], rhs=abt[ab_i][:, fc, 0:N],
                                                        start=(fc == 0), stop=(fc == 3)),
                              reads=[Rwb, R_abt[ab_i]], writes=[R_PS[bpb]], sig=(fc == 3))
                    o3 = 2 * (dc % 2)
                    t_sa, t_sb = acc[:, o3, 0:N], acc[:, o3 + 1, 0:N]
                    Rsa, Rsb = R_acc[o3], R_acc[o3 + 1]
                    kb.op("act", lambda e: e.activation(out=t_sa, in_=PS[bga][:, 0:N], func=AF.Sigmoid,
                                                        bias=bcol[:, 64 + dc:65 + dc], scale=1.0),
                          reads=[R_PS[bga], R_bcol], writes=[Rsa])
                    kb.op("act", lambda e: e.activation(out=t_sb, in_=PS[bgb][:, 0:N], func=AF.Sigmoid,
                                                        bias=bcol[:, 72 + dc:73 + dc], scale=1.0),
                          reads=[R_PS[bgb], R_bcol], writes=[Rsb])
                    if dc + 2 < 8:
                        gg_proj(dc + 2)
                    kb.op("dve", lambda e: e.tensor_tensor(out=t_sa, in0=PS[bpa][:, 0:N], in1=t_sa, op=ALU.mult),
                          reads=[R_PS[bpa], Rsa], writes=[Rsa])
                    kb.op("dve", lambda e: e.tensor_tensor(out=t_sb, in0=PS[bpb][:, 0:N], in1=t_sb, op=ALU.mult),
                          reads=[R_PS[bpb], Rsb], writes=[Rsb])
                    kb.op("pool", lambda e: e.tensor_tensor(out=mT[:, dc, 0:N], in0=t_sa, in1=t_sb, op=ALU.add),
                          reads=[Rsa, Rsb], writes=[R_mT[dc]])
                    if dc == 6 and kind == "p" and ti < 7:
                        vg_pre[0] = (slab_scr(0), slab_scr(8))
                    if dc == 7 and kind == "p" and ti < 7:
                        vg_pre[1] = (slab_scr(1), slab_scr(9))

                if kind == "p":
                    for j in range(4):
                        pend.append((128, j, x[t0 + j * 128: t0 + (j + 1) * 128, :], y[t0 + j * 128: t0 + (j + 1) * 128, :]))
                else:
                    pend.append((32, 0, xs, ys))

            while pend:
                stage4_sub(*pend.pop(0))

            kb.finish()
    return nc


_CACHE = {}


def _tables():
    if "t" in _CACHE:
        return _CACHE["t"]
    slopes = 2.0 ** (-8.0 * np.arange(1, 9, dtype=np.float64) / 8)
    jj = np.arange(128)[:, None]
    ii = np.arange(128)[None, :]
    biasP = np.full((4, 128, 3, 2, 2, 128), NEG, np.float32)
    biasS = np.full((4, 128, 21, 2, 4, 8), NEG, np.float32)
    biasN = np.full((4, 8, 3, 2, 4, 8), NEG, np.float32)
    for hp in range(4):
        for ab in range(2):
            m = slopes[2 * hp + ab]
            for g, (win, d) in enumerate(GROUPS):
                dprev = ii + 128 - jj
                biasP[hp, :, g, ab, 0, :] = np.where(dprev <= 128, -m * dprev * d, NEG)
                dcur = ii - jj
                biasP[hp, :, g, ab, 1, :] = np.where(dcur >= 0, -m * dcur * d, NEG)
                wb = win
                t = np.arange(8)[None, :]
                for rc in range(NCH[g]):
                    r = rc * 128 + np.arange(128)[:, None]
                    dist = wb + t - r
                    ok = (dist % d == 0) & (dist // d <= 128)
                    v = np.where(ok, -m * dist, NEG)
                    biasS[hp, :, CH0[g] + rc, ab, :, :] = v[:, None, :]
                s = np.arange(8)[:, None]
                dist = t - s
                ok = (dist >= 0) & (dist % d == 0)
                v = np.where(ok, -m * dist, NEG)
                biasN[hp, :, g, ab, :, :] = v[:, None, :]
    _CACHE["t"] = (biasP, biasS, biasN)
    return _CACHE["t"]


def kernel(x_prompt, x_sample, cache_kv_w128, cache_kv_w512, cache_kv_w2048, state_conv,
           w_in, b_in, conv_w, conv_b, conv_ln_g, conv_ln_b, w_a, w_b, w_out, ln_g, ln_b):
    f = lambda a: np.ascontiguousarray(np.asarray(a, dtype=np.float32))
    x_prompt, x_sample = f(x_prompt), f(x_sample)
    caches = [f(cache_kv_w128), f(cache_kv_w512), f(cache_kv_w2048)]
    state_conv = f(state_conv)
    w_in, b_in, conv_w = f(w_in), f(b_in), f(conv_w)
    biasP, biasS, biasN = _tables()
    shared = {
        "WIN": f(w_in.reshape(8, 128, 80, 128).transpose(2, 1, 0, 3)),
        "WA": f(f(w_a).reshape(8, 128, 8, 128).transpose(2, 1, 0, 3)),
        "WB": f(f(w_b).reshape(4, 128, 8, 128).transpose(2, 1, 0, 3)),
        "WO": f(f(w_out).reshape(8, 128, 1024).transpose(1, 0, 2)),
        "bcol": f(b_in.reshape(80, 128).T),
        "brep": f(np.broadcast_to(b_in[4608:7680].reshape(1, 24, 1, 128), (128, 24, 4, 128))),
        "convw": f(conv_w.reshape(31, 8, 128).transpose(2, 1, 0)),
        "cvec": f(np.stack([f(conv_b).reshape(8, 128).T, f(conv_ln_g).reshape(8, 128).T,
                            f(conv_ln_b).reshape(8, 128).T], axis=1)),
        "lngb": f(np.broadcast_to(np.stack([f(ln_g), f(ln_b)], 0)[None], (128, 2, 1024))),
        "biasP": biasP, "biasS": biasS, "biasN": biasN, "ident": np.eye(128, dtype=np.float32),
    }
    in_maps = []
    for c in range(8):
        m = dict(shared)
        xp = x_prompt[c]
        m["x"] = xp
        m["xT"] = f(xp.T.reshape(8, 128, 4096).transpose(1, 0, 2))
        xs_ = x_sample[4 * c:4 * c + 4].reshape(32, 1024)
        m["xs"] = f(xs_)
        m["xsT"] = f(xs_.T.reshape(8, 128, 32).transpose(1, 0, 2))
        for g in range(3):
            wb = GROUPS[g][0]
            cg = caches[g][4 * c:4 * c + 4].reshape(4, wb, 2, 512)
            m[f"c{g}"] = f(cg)
            m[f"k{g}"] = f(cg[:, :, 0, :].transpose(0, 2, 1))
        sc = state_conv[4 * c:4 * c + 4]
        m["scT"] = f(sc.reshape(4, 30, 8, 128).transpose(3, 2, 0, 1))
        in_maps.append(m)
    if "nc" not in _CACHE:
        _CACHE["nc"] = build_program()
    res = run_bass_kernel_spmd(_CACHE["nc"], in_maps, core_ids=list(range(8)))
    R = res.results
    y_p = np.stack([R[c]["y"] for c in range(8)], 0)
    y_s = np.concatenate([R[c]["ys"].reshape(4, 8, 1024) for c in range(8)], 0)
    kvp = [np.stack([R[c][f"kvp{g}"].reshape(GROUPS[g][0], 2, 8, 64) for c in range(8)], 0) for g in range(3)]
    conv_p = np.stack([R[c]["convp"].transpose(2, 1, 0).reshape(30, 1024) for c in range(8)], 0)
    kvs = [np.concatenate([R[c][f"kvs{g}"].reshape(4, GROUPS[g][0], 2, 8, 64) for c in range(8)], 0) for g in range(3)]
    conv_s = np.concatenate([R[c]["convs"].transpose(2, 3, 1, 0).reshape(4, 30, 1024) for c in range(8)], 0)
    outs = (y_p, y_s, kvp[0], kvp[1], kvp[2], conv_p, kvs[0], kvs[1], kvs[2], conv_s)
    return tuple(np.ascontiguousarray(o, dtype=np.float32) for o in outs)
```
